# Optimizing a Trainium2 kernel written in Bass

```python
import jax, jax.numpy as jnp
from jax import lax
import numpy as np

D_MODEL = 1024
BATCH = 4
SEQ = 4096
DEPTH = 1
DEC_BATCH = 16
DEC_SEQ = 16
PAST_LEN = 4096

CHUNK = 64
D_MIX = D_MODEL
D_POOL = D_MIX // 2
D_CONV = D_MIX - D_POOL
POOL_WINDOWS = (2, 4, 8, 16)
N_POOL_GROUPS = len(POOL_WINDOWS)
POOL_GROUP = D_POOL // N_POOL_GROUPS
POOL_HIST = max(POOL_WINDOWS) - 1
CONV_WIDTH = 31
CONV_HIST = CONV_WIDTH - 1
D_IN = 2 * D_POOL + 3 * D_CONV
RMS_EPS = 1e-6
LN_EPS = 1e-5

kernel_name = "hymba_pool_conformer_stream"


def rms_norm(x, g):
    x32 = x.astype(jnp.float32)
    y = x32 * lax.rsqrt(jnp.mean(x32 * x32, axis=-1, keepdims=True) + RMS_EPS)
    return (y * g.astype(jnp.float32)).astype(x.dtype)


def layer_norm(x, g, b):
    x32 = x.astype(jnp.float32)
    mu = jnp.mean(x32, axis=-1, keepdims=True)
    xc = x32 - mu
    var = jnp.mean(xc * xc, axis=-1, keepdims=True)
    y = xc * lax.rsqrt(var + LN_EPS) * g.astype(jnp.float32) + b.astype(jnp.float32)
    return y.astype(x.dtype)


def multiscale_pool(u_ext, p0, t_new):
    u32 = u_ext.astype(jnp.float32)
    cs = jnp.cumsum(u32, axis=1)
    cs = jnp.concatenate([jnp.zeros_like(cs[:, :1]), cs], axis=1)
    end = cs[:, POOL_HIST + 1:POOL_HIST + 1 + t_new]
    cur = u32[:, POOL_HIST:]
    pos = p0 + jnp.arange(t_new, dtype=jnp.int32)
    outs = []
    for gi, w in enumerate(POOL_WINDOWS):
        sl = slice(gi * POOL_GROUP, (gi + 1) * POOL_GROUP)
        start = cs[:, POOL_HIST + 1 - w:POOL_HIST + 1 - w + t_new, sl]
        cnt = jnp.minimum(pos + 1, w).astype(jnp.float32)[None, :, None]
        outs.append((end[..., sl] - start) / cnt - cur[..., sl])
    return jnp.concatenate(outs, axis=-1).astype(u_ext.dtype)


def causal_dwconv(v_ext, w, b):
    y = lax.conv_general_dilated(
        v_ext, w[:, None, :].astype(v_ext.dtype), window_strides=(1,), padding='VALID',
        dimension_numbers=('NWC', 'WIO', 'NWC'), feature_group_count=v_ext.shape[-1])
    return y + b


def mixer_layer(x, pool_hist, conv_hist, p0, norm_g, w_in, pool_mix, pool_scale,
                dw_w, dw_b, ln_g, ln_b, pw_w, pw_b, w_out):
    B, T, _ = x.shape
    h = rms_norm(x, norm_g)
    proj = jnp.einsum('btd,de->bte', h, w_in)
    u, g_pool, v_a, v_b, g_conv = jnp.split(
        proj, [D_POOL, 2 * D_POOL, 2 * D_POOL + D_CONV, 2 * D_POOL + 2 * D_CONV], axis=-1)
    u_ext = jnp.concatenate([pool_hist.astype(u.dtype), u], axis=1)
    pooled = multiscale_pool(u_ext, p0, T)
    pooled = jnp.einsum('btgc,gcd->btgd', pooled.reshape(B, T, N_POOL_GROUPS, POOL_GROUP),
                        pool_mix).reshape(B, T, D_POOL) * pool_scale
    pool_out = jax.nn.silu(g_pool) * pooled
    v = v_a * jax.nn.sigmoid(v_b)
    v_ext = jnp.concatenate([conv_hist.astype(v.dtype), v], axis=1)
    c = causal_dwconv(v_ext, dw_w, dw_b)
    c = jax.nn.silu(layer_norm(c, ln_g, ln_b))
    c = jnp.einsum('btc,ce->bte', c, pw_w) + pw_b
    conv_out = jax.nn.silu(g_conv) * c
    y = x + jnp.einsum('bte,ed->btd', jnp.concatenate([pool_out, conv_out], axis=-1), w_out)
    return y, u_ext[:, -POOL_HIST:], v_ext[:, -CONV_HIST:]


def setup_inputs(seed: int = 0) -> dict:
    key = jax.random.key(seed)
    ks = jax.random.split(key, 16)
    f32 = jnp.float32
    nrm = lambda k, s: jax.random.normal(k, s, f32)
    return {
        'x_prompt': nrm(ks[0], (BATCH, SEQ, D_MODEL)),
        'x_sample': nrm(ks[1], (DEC_BATCH, DEC_SEQ, D_MODEL)),
        'cache_pool': nrm(ks[2], (DEPTH, DEC_BATCH, POOL_HIST, D_POOL)),
        'cache_conv': 0.5 * nrm(ks[3], (DEPTH, DEC_BATCH, CONV_HIST, D_CONV)),
        'norm_g': 1.0 + 0.05 * nrm(ks[4], (DEPTH, D_MODEL)),
        'w_in': nrm(ks[5], (DEPTH, D_MODEL, D_IN)) * D_MODEL ** -0.5,
        'pool_mix': nrm(ks[6], (DEPTH, N_POOL_GROUPS, POOL_GROUP, POOL_GROUP)) * POOL_GROUP ** -0.5,
        'pool_scale': 1.0 + 0.1 * nrm(ks[7], (DEPTH, D_POOL)),
        'dw_w': nrm(ks[8], (DEPTH, CONV_WIDTH, D_CONV)) * CONV_WIDTH ** -0.5,
        'dw_b': 0.02 * nrm(ks[9], (DEPTH, D_CONV)),
        'ln_g': 1.0 + 0.05 * nrm(ks[10], (DEPTH, D_CONV)),
        'ln_b': 0.02 * nrm(ks[11], (DEPTH, D_CONV)),
        'pw_w': nrm(ks[12], (DEPTH, D_CONV, D_CONV)) * D_CONV ** -0.5,
        'pw_b': 0.02 * nrm(ks[13], (DEPTH, D_CONV)),
        'w_out': nrm(ks[14], (DEPTH, D_MIX, D_MODEL)) * D_MIX ** -0.5,
        'final_g': 1.0 + 0.05 * nrm(ks[15], (D_MODEL,)),
    }


def reference(x_prompt, x_sample, cache_pool, cache_conv, norm_g, w_in, pool_mix, pool_scale,
              dw_w, dw_b, ln_g, ln_b, pw_w, pw_b, w_out, final_g):
    yp = x_prompt
    ys = x_sample
    bp = x_prompt.shape[0]
    sp_pool, sp_conv, ss_pool, ss_conv = [], [], [], []
    for l in range(DEPTH):
        lw = (norm_g[l], w_in[l], pool_mix[l], pool_scale[l], dw_w[l], dw_b[l],
              ln_g[l], ln_b[l], pw_w[l], pw_b[l], w_out[l])
        zero_pool = jnp.zeros((bp, POOL_HIST, D_POOL), x_prompt.dtype)
        zero_conv = jnp.zeros((bp, CONV_HIST, D_CONV), x_prompt.dtype)
        yp, hp_pool, hp_conv = mixer_layer(yp, zero_pool, zero_conv, 0, *lw)
        ys, hs_pool, hs_conv = mixer_layer(ys, cache_pool[l], cache_conv[l], PAST_LEN, *lw)
        sp_pool.append(hp_pool)
        sp_conv.append(hp_conv)
        ss_pool.append(hs_pool)
        ss_conv.append(hs_conv)
    y_prompt = rms_norm(yp, final_g)
    y_sample = rms_norm(ys, final_g)
    return (y_prompt, y_sample, jnp.stack(sp_pool), jnp.stack(sp_conv),
            jnp.stack(ss_pool), jnp.stack(ss_conv))
```

```python
from contextlib import ExitStack

import numpy as np
import concourse.bass as bass
import concourse.mybir as mybir
from concourse.bass_utils import run_bass_kernel_spmd

F32 = mybir.dt.float32
BF16 = mybir.dt.bfloat16
ALU = mybir.AluOpType
AF = mybir.ActivationFunctionType

NT = 256
NPOS = 9
WINS = (2, 4, 8, 16)
NTAP = 31
EPS_RMS = 1e-6
EPS_LN = 1e-5
NPP = 313


class Sched:
    def __init__(self, nc):
        self.nc = nc
        self.ops = []
        self.last_w = {}
        self.readers = {}
        self.last_dma = {}
        self.cur_batch = {}
        self.batch_last = {}

    def op(self, eng, fn, reads=(), writes=(), dma=None, batch=None):
        idx = len(self.ops)
        deps = set()
        for k in reads:
            if k in self.last_w:
                deps.add(self.last_w[k])
        for k in writes:
            if k in self.last_w:
                deps.add(self.last_w[k])
            deps.update(self.readers.get(k, ()))
        same_batch = (batch is not None and self.cur_batch.get(dma) == batch)
        if dma is not None and dma in self.last_dma and not same_batch:
            deps.add(self.last_dma[dma])
        deps.discard(idx)
        self.ops.append(dict(eng=eng, fn=fn, deps=deps, dma=dma, batch=batch))
        if batch is not None:
            self.cur_batch[dma] = batch
            self.batch_last[(dma, batch)] = idx
        if dma is not None:
            self.last_dma[dma] = idx
        for k in writes:
            self.last_w[k] = idx
            self.readers[k] = set()
        for k in reads:
            self.readers.setdefault(k, set()).add(idx)
        return idx

    def emit(self, stack):
        nc = self.nc
        engs = {'pe': nc.tensor, 'act': nc.scalar, 'dve': nc.vector,
                'pool': nc.gpsimd, 'sp': nc.sync}
        ops = self.ops
        pos = {}
        cnt = {}
        for i, o in enumerate(ops):
            st = ('dma', o['dma']) if o['dma'] is not None else o['eng']
            o['stream'] = st
            cnt[st] = cnt.get(st, 0) + 1
            pos[i] = cnt[st]
        waited = {}
        signal = set()
        for i, o in enumerate(ops):
            need = {}
            for d in o['deps']:
                if ops[d]['batch'] is not None:
                    if o['batch'] == ops[d]['batch'] and o['dma'] == ops[d]['dma']:
                        continue
                    d = self.batch_last[(ops[d]['dma'], ops[d]['batch'])]
                sd = ops[d]['stream']
                if sd == 'pe' and o['eng'] == 'pe' and o['dma'] is None:
                    continue
                if sd not in need or pos[d] > pos[need[sd]]:
                    need[sd] = d
            w = []
            for sd, d in need.items():
                key = (o['eng'], sd)
                if waited.get(key, 0) < pos[d]:
                    waited[key] = pos[d]
                    w.append(d)
                    signal.add(d)
            o['waits'] = w
        val = {}
        run = {}
        for i, o in enumerate(ops):
            st = o['stream']
            if isinstance(st, tuple):
                run[st] = run.get(st, 0) + 16
                val[i] = run[st]
            elif i in signal:
                run[st] = run.get(st, 0) + 1
                val[i] = run[st]
        sems = {}
        for st in cnt:
            nm = ('d_' + st[1]) if isinstance(st, tuple) else ('e_' + st)
            sems[st] = stack.enter_context(nc.semaphore(nm))
        nwait = 0
        for i, o in enumerate(ops):
            e = engs[o['eng']]
            ws = list(o['waits'])
            attach = ws.pop() if (ws and o['fn'] is not None) else None
            for d in ws:
                e.wait_ge(sems[ops[d]['stream']], val[d])
                nwait += 1
            if o['fn'] is None:
                continue
            ins = o['fn'](e)
            if attach is not None:
                ins._wait_ge(sems[ops[attach]['stream']], val[attach])
            st = o['stream']
            if isinstance(st, tuple):
                ins.then_inc(sems[st], 16)
            elif i in signal:
                ins.then_inc(sems[st], 1)
        return dict(n_ops=len(ops), n_waits=nwait, n_signal=len(signal))


def build_program():
    nc = bass.Bass("TRN2", target_bir_lowering=False)

    def din(name, shape):
        return nc.dram_tensor(name, list(shape), F32, kind="ExternalInput").ap()

    def dout(name, shape):
        return nc.dram_tensor(name, list(shape), F32, kind="ExternalOutput").ap()

    xp_d = din("xp", (2048, 1024))
    xh_d = din("xh", (64, 1024))
    cpool_d = din("cpool", (30, 512))
    cconv_d = din("cconv", (60, 512))
    win_d = din("w_in", (1024, 2560))
    wout_d = din("w_out", (1024, 1024))
    pww_d = din("pw_w", (512, 512))
    pmw_d = din("pool_mix", (512, 128))
    fg_d = din("final_g", (1, 1024))
    ng_d = din("norm_g", (1, 1024))
    pp_d = din("pp", (128, NPP))
    id_d = din("ident", (128, 128))
    yp_d = dout("yp", (2048, 1024))
    ys_d = dout("ys", (32, 1024))
    sp_pool_d = dout("sp_pool", (15, 512))
    sp_conv_d = dout("sp_conv", (30, 512))
    ss_pool_d = dout("ss_pool", (30, 512))
    ss_conv_d = dout("ss_conv", (60, 512))

    with ExitStack() as st:
        def sb(n, s, d):
            return st.enter_context(nc.sbuf_tensor(n, list(s), d))

        def ps(n, s, d):
            return st.enter_context(nc.psum_tensor(n, list(s), d))

        WIN = sb("WIN", (128, 8, 2560), BF16)
        WOUT = sb("WOUT", (128, 8, 1024), BF16)
        PWW = sb("PWW", (128, 4, 512), BF16)
        PMW = sb("PMW", (128, 4, 128), BF16)
        WS = sb("WS", (128, 16, 8, 32), BF16)
        VRS = [sb("VR%d" % s_, (128, 16, NT + 8), BF16) for s_ in range(2)]
        VRH = sb("VRH", (128, 4, 4, 2, 48), BF16)
        GB = sb("GB", (128, 1024), F32)
        ONES = sb("ONES", (128, 128), BF16)
        ID32 = sb("ID32", (128, 128), F32)
        IDB = sb("IDB", (128, 128), BF16)
        FGB = sb("FGB", (128, 1024), F32)
        PP = sb("PP", (128, NPP), F32)
        DWS = sb("DWS", (128, 128), F32)
        NEGH = sb("NEGH", (128, 8), F32)
        STAT = sb("STAT", (128, 80), F32)
        IOT = sb("IOT", (128, 16), F32)
        CNT = sb("CNT", (128, 16), F32)
        INV = sb("INV", (128, 4, 16), F32)
        T16 = sb("T16", (128, 16), F32)
        XB = [sb("XB%d" % s, (128, 2, 1024), F32) for s in range(3)]
        JUNKS = [sb("JUNK%d" % j, (128, 1024), BF16) for j in range(1)]
        HG = [sb("HG%d" % h, (128, 1024), BF16) for h in range(3)]
        HT = sb("HT", (128, 8, NT), BF16)
        HTH = sb("HTH", (128, 8, 64), BF16)
        THH = sb("THH", (128, 4, 64), F32)
        U = [sb("U%d" % s, (128, 4, 16 + NT), F32) for s in range(2)]
        V = [sb("V%d" % s, (128, 4, 32 + NT), BF16) for s in range(2)]
        US = sb("US", (128, 4, 2, 32), F32)
        VS32 = sb("VS32", (128, 4, 2, 48), F32)
        VSB = sb("VSB", (128, 4, 2, 48), BF16)
        V32T = sb("V32T", (128, 4, 32), F32)
        SGP = [sb("SGP%d" % s, (128, 4, NT), F32) for s in range(2)]
        SGC = [sb("SGC%d" % s, (128, 4, NT), F32) for s in range(2)]
        TH = [sb("TH%d" % s, (128, NT), F32) for s in range(4)]
        PA = sb("PA", (128, 16 + NT), F32)
        PB = sb("PB", (128, 16 + NT), F32)
        PLS = [sb("PL%d" % s, (128, 4, NT), BF16) for s in range(2)]
        CO = sb("CO", (128, 4, NT), F32)
        CBS = [sb("CBS%d" % s, (128, 2, NT), BF16) for s in range(2)]
        MUS = sb("MUS", (128, NT), F32)
        MSQ = sb("MSQ", (128, NT), F32)
        LNS = sb("LNS", (128, 64), F32)
        DSC = sb("DSC", (128, 128), F32)
        LNO = sb("LNO", (128, 4, NT), BF16)
        MIXP = [sb("MIXP%d" % s, (128, 4, NT), BF16) for s in range(2)]
        MIXC = sb("MIXC", (128, 4, NT), BF16)
        STA = sb("STA", (64, 512), F32)
        STB = sb("STB", (64, 512), F32)
        GEN = [ps("GEN%d" % s, (128, 512), F32) for s in range(5)]
        GENB = [g.bitcast(BF16) for g in GEN]
        STP = ps("STP", (128, 512), F32)
        OB = [ps("OB%d" % s, (128, 512), F32) for s in range(2)]

        S = Sched(nc)
        cnt = dict(acc=0, cv=0, o=0, tp=0, hg=0, th=0, stg=0, cb=0, junk=0)

        def junk_tile():
            return JUNKS[0], 'JUNK0'

        def acc_tile():
            i = cnt['acc'] % 5
            cnt['acc'] += 1
            return GEN[i][:, 0:NT], "GEN%d" % i

        def tp_tile():
            i = cnt['acc'] % 5
            cnt['acc'] += 1
            return GENB[i], "GEN%d" % i

        cv_tile = acc_tile

        def o_tile():
            i = cnt['o'] % 2
            cnt['o'] += 1
            return OB[i], "O%d" % i

        def statc(i, j, kind):
            c = (i * 2 + j) * 4 + kind
            return STAT[:, c:c + 1], "ST%d" % c

        def r3(ap, a):
            return ap.rearrange("p (a b) -> p a b", a=a)

        def wload(dst, src, key, chan):
            S.op('pool', lambda e: e.dma_start(out=dst, in_=src), writes=[key], dma=chan)

        def load_win():
            wv = win_d.rearrange("(k p) e -> p k e", p=128)
            n_ = 0
            for seg in (3, 2, 0, 1, 4):
                for hf in range(2):
                    lo = seg * 512 + hf * 256
                    wload(WIN[:, :, lo:lo + 256], wv[:, :, lo:lo + 256], 'WINs%d_%d' % (seg, hf), 'w%d' % n_)
                    n_ += 1

        def load(dst, src, key, chan):
            S.op('sp', lambda e: e.dma_start(out=dst, in_=src), writes=[key], dma=chan)

        load_win()
        load(PP[:], pp_d, 'PP', 'c_pp')
        load(ID32[:], id_d, 'ID32', 'c_id')
        load(XB[0][0:64, 0, :], xh_d, 'X0j0', 'ldx0')
        load(STA[0:30, :], cpool_d, 'STA', 'c_sta')
        load(STB[0:60, :], cconv_d, 'STB', 'c_stb')
        S.op('pool', lambda e: e.memset(STAT[:], 0.0), writes=['STAT'])
        S.op('pool', lambda e: e.memset(NEGH[:], -0.5), writes=['NEGH'])
        S.op('pool', lambda e: e.memset(LNS[:], 0.0), writes=['LNS'])
        S.op('pool', lambda e: e.memset(VS32[:], 0.0), writes=['VS32h'] + ['VS32n%d' % c for c in range(4)] + ['VS32h_%d' % c for c in range(3)])
        S.op('pool', lambda e: e.memset(US[:], 0.0), writes=['USn%d' % c for c in range(4)] + ['USh%d' % c for c in range(4)])
        S.op('pool', lambda e: e.memset(ONES[:], 1.0 / 512.0), writes=['ONES'])
        S.op('pool', lambda e: e.iota(IOT[:], [[1, 16]], base=0, channel_multiplier=0,
                                      allow_small_or_imprecise_dtypes=True), writes=['IOT'])
        S.op('dve', lambda e: e.tensor_copy(out=IDB[:], in_=ID32[:]), reads=['ID32'], writes=['IDB'])
        S.op('dve', lambda e: e.tensor_scalar(out=DWS[:], in0=PP[:, 153:281], scalar1=0.5, scalar2=None,
                                              op0=ALU.mult), reads=['PP'], writes=['DWS'])
        load(GB[:], ng_d.partition_broadcast(128), 'GB', 'c_gb')
        S.op('dve', lambda e: e.tensor_scalar(out=CNT[:], in0=IOT[:], scalar1=PP[:, 152:153], scalar2=1.0,
                                              op0=ALU.add, op1=ALU.add), reads=['IOT', 'PP'], writes=['CNT'])
        for m, w in enumerate(WINS):
            S.op('dve', lambda e, m=m, w=w: e.tensor_scalar(out=INV[:, m, :], in0=CNT[:], scalar1=float(w),
                                                            scalar2=None, op0=ALU.min),
                 reads=['CNT'], writes=['INV%d' % m])
            S.op('dve', lambda e, m=m: e.reciprocal(out=INV[:, m, :], in_=INV[:, m, :]),
                 reads=['INV%d' % m], writes=['INV%d' % m])

        PROJ_ORDER = [12, 13, 14, 15, 8, 9, 10, 11, 0, 1, 2, 3, 4, 5, 6, 7, 16, 17, 18, 19]

        def load_misc_weights_a():
            wload(PMW[:], pmw_d.rearrange("(g p) d -> p g d", p=128), 'PMW', 'w0')
            wload(PWW[:], pww_d.rearrange("(c p) e -> p c e", p=128), 'PWW', 'w1')

        def load_wout():
            wo = wout_d.rearrange("(k p) e -> p k e", p=128)
            for h in range(2):
                wload(WOUT[:, 4 * h:4 * h + 4, :], wo[:, 4 * h:4 * h + 4, :], 'WOUTh%d' % h, 'w%d' % (2 - h))
            load(FGB[:], fg_d.partition_broadcast(128), 'FGB', 'c_fg')

        def build_ws():
            S.op('dve', lambda e: e.tensor_tensor(
                out=WS[:].rearrange("p j r m -> p (j r) m"),
                in0=PP[:, 281:313].unsqueeze(1).to_broadcast([128, 128, 32]),
                in1=DWS[:].unsqueeze(2).to_broadcast([128, 128, 32]), op=ALU.mult),
                reads=['PP', 'DWS'], writes=['WS'])

        def replicate_v(i):
            s2 = i % 2
            for sh in range(4):
                for q in range(4):
                    if is_h(i):
                        src = VSB[32 * q:32 * q + 32, :, :, 8 * sh:8 * sh + 24].rearrange("p c s x -> p (c s) x")
                        dst = VRH[32 * sh:32 * sh + 32, q, :, :, 0:24].rearrange("p c s x -> p (c s) x")
                        rk, wk, ch = ['VSB'], 'VRH', 'vrh'
                    else:
                        src = V[s2][32 * q:32 * q + 32, :, 8 * sh:8 * sh + NT + 8]
                        dst = VRS[s2][32 * sh:32 * sh + 32, :, :].rearrange("p (c q) x -> p c q x", q=4)[:, :, q, :]
                        rk = ['V%dc%d' % (s2, c) for c in range(4)] + ['V%dm' % s2]
                        wk, ch = 'VRa%d' % s2, 'vrp'
                    S.op('pool', lambda e, src=src, dst=dst: e.dma_start(out=dst, in_=src),
                         reads=rk, writes=[wk], dma=ch, batch=i)

        def is_h(i):
            return i == 0

        def load_x(i):
            g = i - 1
            s = i % 3
            src = xp_d[g * NT:(g + 1) * NT, :].rearrange("(j p) d -> p j d", p=128)
            if i == 1:
                for j, eng in ((0, 'sp'), (1, 'act')):
                    S.op(eng, lambda e, j=j: e.dma_start(out=XB[s][:, j, :], in_=src[:, j, :]),
                         writes=['X%dj%d' % (s, j)], dma='ldx%d_%d' % (s, j))
                return
            S.op('sp', lambda e: e.dma_start(out=XB[s][:], in_=src),
                 writes=['X%dj0' % s, 'X%dj1' % s], dma='ldx%d' % s)

        hg_of = {}

        def stageI_pre(i):
            s = i % 3
            npart = 64 if is_h(i) else 128
            subs = []
            for j in range(1 if is_h(i) else 2):
                h = cnt['hg'] % 3
                cnt['hg'] += 1
                hg_of[(i, j)] = h
                subs.append((j, XB[s][0:npart, j, :], 'X%dj%d' % (s, j), statc(i, j, 0), statc(i, j, 1), h))
            for j, xt, xk, (ss, ssk), (rs, rsk), h in subs:
                jt, jk = junk_tile()
                S.op('act', lambda e, xt=xt, ss=ss, jt=jt: e.activation(
                    out=jt[0:npart, :], in_=xt, func=AF.Square, scale=1.0 / 32.0, accum_out=ss[0:npart, :]),
                    reads=[xk, 'STAT'], writes=[ssk, jk])
            for j, xt, xk, (ss, ssk), (rs, rsk), h in subs:
                S.op('dve', lambda e, ss=ss: e.tensor_scalar(
                    out=ss[0:npart, :], in0=ss[0:npart, :], scalar1=EPS_RMS, scalar2=None, op0=ALU.add),
                    reads=[ssk], writes=[ssk])
            for j, xt, xk, (ss, ssk), (rs, rsk), h in subs:
                S.op('pool', lambda e, ss=ss, rs=rs: e.tensor_tensor(
                    out=rs[0:npart, :], in0=ss[0:npart, :], in1=NEGH[0:npart, 0:1], op=ALU.pow),
                    reads=[ssk, 'NEGH', 'STAT'], writes=[rsk])
            for j, xt, xk, (ss, ssk), (rs, rsk), h in subs:
                S.op('dve', lambda e, xt=xt, rs=rs, h=h: e.scalar_tensor_tensor(
                    out=HG[h][0:npart, :], in0=xt, scalar=rs[0:npart, :], in1=GB[0:npart, :],
                    op0=ALU.mult, op1=ALU.mult), reads=[xk, rsk, 'GB'], writes=['HG%d' % h])

        def stageI_pe(i):
            nsub = 1 if is_h(i) else 2
            tw = 64 if is_h(i) else 128
            for j in range(nsub):
                h = hg_of[(i, j)]
                for half in range(1 if is_h(i) else 2):
                    nk = 8 if is_h(i) else 4
                    i_ = cnt['acc'] % 5
                    cnt['acc'] += 1
                    tp = GEN[i_]
                    tpk = "GEN%d" % i_
                    for kk in range(nk):
                        k = half * nk + kk
                        S.op('pe', lambda e, k=k, kk=kk, h=h, tp=tp: e.matmul(
                            tp[:, kk * tw:(kk + 1) * tw], lhsT=HG[h][0:tw, k * 128:(k + 1) * 128],
                            rhs=IDB[0:tw, 0:tw], start=True, stop=True),
                            reads=['HG%d' % h, 'IDB'], writes=[tpk])
                    hts = HTH if is_h(i) else HT
                    S.op('act', lambda e, tp=tp, j=j, half=half, nk=nk, hts=hts: e.activation(
                        out=hts[:, half * nk:(half + 1) * nk, j * tw:(j + 1) * tw], in_=r3(tp[:, 0:nk * tw], nk),
                        func=AF.Copy), reads=[tpk], writes=['HTH'] if is_h(i) else ['HTj%d_%d' % (j, half)])

        th_of = {}

        def stageP(i, rep=True, segs=(0, 1, 2, 3, 4), tail=True):
            n = 64 if is_h(i) else NT
            s2 = i % 2
            sn = (i + 1) % 2
            htk = ['HTH'] if is_h(i) else ['HTj0_0', 'HTj0_1', 'HTj1_0', 'HTj1_1']
            hts = HTH if is_h(i) else HT
            last = (i == NPOS - 1)
            th_cur = th_of.setdefault(i, {})
            if tuple(segs) == (0, 1, 2):
                order = [12, 8, 13, 9, 14, 10, 15, 11, 0, 1, 2, 3]
            elif len(segs) == 5:
                order = [12, 8, 13, 9, 14, 10, 15, 11] + PROJ_ORDER[8:]
            else:
                order = [m_ for sg in segs for m_ in PROJ_ORDER[4 * sg:4 * sg + 4]]
            for m in order:
                acc, acck = acc_tile()
                for k in range(8):
                    S.op('pe', lambda e, k=k, m=m, acc=acc: e.matmul(
                        acc[:, 0:n], lhsT=WIN[:, k, m * 128:(m + 1) * 128], rhs=hts[:, k, 0:n],
                        start=(k == 0), stop=(k == 7)), reads=htk + ['WINs%d_%d' % (m // 4, (m % 4) // 2)], writes=[acck])
                if 12 <= m < 16:
                    c = m - 12
                    if is_h(i):
                        t = 'H%d' % c
                        tht = THH[:, c, :]
                    else:
                        t = cnt['th'] % 4
                        cnt['th'] += 1
                        tht = TH[t]
                    th_cur[c] = (t, tht)
                    S.op('act', lambda e, acc=acc, tht=tht: e.activation(
                        out=tht[:, 0:n], in_=acc[:, 0:n], func=AF.Tanh, scale=0.5),
                        reads=[acck], writes=['TH%s' % t])
                elif 8 <= m < 12:
                    c = m - 8
                    t, tht = th_cur[c]
                    if is_h(i):
                        S.op('dve', lambda e, acc=acc, tht=tht, c=c: e.scalar_tensor_tensor(
                            out=VS32[:, c, :, 32:48], in0=r3(tht[:, 0:32], 2), scalar=1.0,
                            in1=r3(acc[:, 0:32], 2), op0=ALU.add, op1=ALU.mult),
                            reads=[acck, 'TH%s' % t], writes=['VS32n%d' % c])
                        S.op('dve', lambda e, acc=acc, tht=tht, c=c: e.scalar_tensor_tensor(
                            out=V[sn][:, c, 0:32], in0=tht[:, 32:64], scalar=1.0,
                            in1=acc[:, 32:64], op0=ALU.add, op1=ALU.mult),
                            reads=[acck, 'TH%s' % t], writes=['V%dm' % sn])
                    else:
                        S.op('dve', lambda e, acc=acc, tht=tht, c=c: e.scalar_tensor_tensor(
                            out=V[s2][:, c, 32:32 + NT], in0=tht[:], scalar=1.0,
                            in1=acc, op0=ALU.add, op1=ALU.mult),
                            reads=[acck, 'TH%s' % t], writes=['V%dc%d' % (s2, c)])
                        if last:
                            S.op('dve', lambda e, acc=acc, tht=tht, c=c: e.scalar_tensor_tensor(
                                out=V32T[:, c, :], in0=tht[:, NT - 32:NT], scalar=1.0,
                                in1=acc[:, NT - 32:NT], op0=ALU.add, op1=ALU.mult),
                                reads=[acck, 'TH%s' % t], writes=['V32T%d' % c])
                    if m == 11 and rep and not is_h(i):
                        replicate_v(i)
                elif m < 4:
                    if is_h(i):
                        S.op('act', lambda e, acc=acc, m=m: e.activation(
                            out=US[:, m, :, 16:32], in_=r3(acc[:, 0:32], 2), func=AF.Copy),
                            reads=[acck], writes=['USn%d' % m])
                        S.op('act', lambda e, acc=acc, m=m: e.activation(
                            out=U[sn][:, m, 0:16], in_=acc[:, 48:64], func=AF.Copy),
                            reads=[acck], writes=['U%dm' % sn])
                    else:
                        S.op('act', lambda e, acc=acc, m=m: e.activation(
                            out=U[s2][:, m, 16:16 + NT], in_=acc, func=AF.Copy),
                            reads=[acck], writes=['U%dc%d' % (s2, m)])
                elif m < 8:
                    mm_ = m - 4
                    nn = 32 if is_h(i) else NT
                    S.op('act', lambda e, acc=acc, mm_=mm_, nn=nn: e.activation(
                        out=SGP[s2][:, mm_, 0:nn], in_=acc[:, 0:nn], func=AF.Silu),
                        reads=[acck], writes=['SGP%d_%d' % (s2, mm_)])
                else:
                    mm_ = m - 16
                    nn = 32 if is_h(i) else NT
                    S.op('act', lambda e, acc=acc, mm_=mm_, nn=nn: e.activation(
                        out=SGC[s2][:, mm_, 0:nn], in_=acc[:, 0:nn], func=AF.Silu),
                        reads=[acck], writes=['SGC%d_%d' % (s2, mm_)])
            if not tail:
                return
            if is_h(i):
                S.op('dve', lambda e: e.tensor_copy(out=VSB[:], in_=VS32[:]),
                     reads=['VS32n%d' % c for c in range(4)] + ['VS32h'], writes=['VSB'])
            if is_h(i):
                pass
            elif not last:
                S.op('pool', lambda e: e.tensor_copy(out=V[sn][:, :, 0:32], in_=V[s2][:, :, NT:NT + 32]),
                     reads=['V%dc%d' % (s2, c) for c in range(4)], writes=['V%dm' % sn])
                S.op('pool', lambda e: e.tensor_copy(out=U[sn][:, :, 0:16], in_=U[s2][:, :, NT:NT + 16]),
                     reads=['U%dc%d' % (s2, c) for c in range(4)], writes=['U%dm' % sn])

        def cache_prep():
            for m in range(4):
                acc, acck = acc_tile()
                S.op('pe', lambda e, acc=acc, m=m: e.transpose(
                    out=acc[:, 0:30], in_=STA[0:30, m * 128:(m + 1) * 128], identity=ID32[0:30, 0:30]),
                    reads=['STA', 'ID32'], writes=[acck])
                S.op('act', lambda e, acc=acc, m=m: e.activation(
                    out=US[:, m, :, 1:16], in_=r3(acc[:, 0:30], 2), func=AF.Copy),
                    reads=[acck], writes=['USh%d' % m])
            for c in range(4):
                acc, acck = acc_tile()
                S.op('pe', lambda e, acc=acc, c=c: e.transpose(
                    out=acc[:, 0:60], in_=STB[0:60, c * 128:(c + 1) * 128], identity=ID32[0:60, 0:60]),
                    reads=['STB', 'ID32'], writes=[acck])
                S.op('act', lambda e, acc=acc, c=c: e.activation(
                    out=VS32[:, c, :, 2:32], in_=r3(acc[:, 0:60], 2), func=AF.Copy, scale=2.0),
                    reads=[acck], writes=['VS32h'] if c == 3 else ['VS32h_%d' % c])

        def stagePOOL(i):
            s2 = i % 2
            PL = PLS[s2]
            if is_h(i):
                def uu(m, lo, hi):
                    return US[:, m, :, lo:hi]

                def pa(lo, hi):
                    return r3(PA[:, 0:64], 2)[:, :, lo:hi]

                def pb(lo, hi):
                    return r3(PB[:, 0:64], 2)[:, :, lo:hi]
                W = 32
                ukeys = lambda m: ['USn%d' % m, 'USh%d' % m]
                plv = lambda m: r3(PL[:, m, 0:32], 2)
            else:
                def uu(m, lo, hi):
                    return U[s2][:, m, lo:hi]

                def pa(lo, hi):
                    return PA[:, lo:hi]

                def pb(lo, hi):
                    return PB[:, lo:hi]
                W = 16 + NT
                ukeys = lambda m: ['U%dc%d' % (s2, m), 'U%dm' % s2]
                plv = lambda m: PL[:, m, :]

            def add(out, a, b, reads, writes):
                S.op('dve', lambda e: e.tensor_tensor(out=out, in0=a, in1=b, op=ALU.add),
                     reads=reads, writes=writes)
            for m, w in enumerate(WINS):
                uk = ukeys(m)
                if w == 2:
                    add(pa(16, W), uu(m, 16, W), uu(m, 15, W - 1), uk, ['PA'])
                    fin, fk = pa, 'PA'
                elif w == 4:
                    add(pa(14, W), uu(m, 14, W), uu(m, 13, W - 1), uk, ['PA'])
                    add(pb(16, W), pa(16, W), pa(14, W - 2), ['PA'], ['PB'])
                    fin, fk = pb, 'PB'
                elif w == 8:
                    add(pa(10, W), uu(m, 10, W), uu(m, 9, W - 1), uk, ['PA'])
                    add(pb(12, W), pa(12, W), pa(10, W - 2), ['PA'], ['PB'])
                    add(pa(16, W), pb(16, W), pb(12, W - 4), ['PB'], ['PA'])
                    fin, fk = pa, 'PA'
                else:
                    add(pa(2, W), uu(m, 2, W), uu(m, 1, W - 1), uk, ['PA'])
                    add(pb(4, W), pa(4, W), pa(2, W - 2), ['PA'], ['PB'])
                    add(pa(8, W), pb(8, W), pb(4, W - 4), ['PB'], ['PA'])
                    add(pb(16, W), pa(16, W), pa(8, W - 8), ['PA'], ['PB'])
                    fin, fk = pb, 'PB'
                S.op('dve', lambda e, m=m, w=w, fin=fin: e.scalar_tensor_tensor(
                    out=plv(m), in0=fin(16, W), scalar=1.0 / w, in1=uu(m, 16, W),
                    op0=ALU.mult, op1=ALU.subtract), reads=[fk] + uk, writes=['PL%d_%d' % (s2, m)])
                if i == 1:
                    S.op('dve', lambda e, m=m, fin=fin: e.tensor_tensor(
                        out=T16[:], in0=fin(16, 32), in1=INV[:, m, :], op=ALU.mult),
                        reads=[fk, 'INV%d' % m], writes=['T16'])
                    S.op('dve', lambda e, m=m: e.tensor_tensor(
                        out=PL[:, m, 0:16], in0=T16[:], in1=uu(m, 16, 32), op=ALU.subtract),
                        reads=['T16'] + uk, writes=['PL%d_%d' % (s2, m)])

        def stagePM(i):
            s2 = i % 2
            n = 32 if is_h(i) else NT
            for m in range(4):
                acc, acck = acc_tile()
                S.op('pe', lambda e, acc=acc, m=m: e.matmul(
                    acc[:, 0:n], lhsT=PMW[:, m, :], rhs=PLS[s2][:, m, 0:n], start=True, stop=True),
                    reads=['PMW', 'PL%d_%d' % (s2, m)], writes=[acck])
                S.op('dve', lambda e, acc=acc, m=m: e.scalar_tensor_tensor(
                    out=MIXP[s2][:, m, 0:n], in0=acc[:, 0:n], scalar=PP[:, m:m + 1], in1=SGP[s2][:, m, 0:n],
                    op0=ALU.mult, op1=ALU.mult), reads=[acck, 'PP', 'SGP%d_%d' % (s2, m)],
                    writes=['MIXP%d_%d' % (s2, m)])

        def stageCONV(i, pm=True):
            s2 = i % 2
            n = 32 if is_h(i) else NT
            pend = None

            def stats(c, cb):
                S.op('pe', lambda e: e.matmul(STP[:, 0:2 * n], lhsT=ONES[:], rhs=CBS[cb][:, :, 0:n],
                                              start=(c == 0), stop=(c == 3)),
                     reads=['ONES', 'CBSa%d' % cb, 'CBSb%d' % cb], writes=['STP'])
            for c in range(4):
                cv, cvk = cv_tile()
                for r in range(8):
                    for q in range(4):
                        j = 4 * c + q
                        if is_h(i):
                            S.op('pe', lambda e, cv=cv, j=j, r=r, q=q, c=c: e.matmul(
                                r3(cv[32 * q:32 * q + 32, 0:32], 2), lhsT=WS[:, j, r, :],
                                rhs=VRH[:, q, c, :, 1 + r:1 + r + 16],
                                start=(r == 0), stop=(r == 7), tile_position=(0, 32 * q)),
                                reads=['WS', 'VRH'], writes=[cvk])
                        else:
                            S.op('pe', lambda e, cv=cv, j=j, r=r, q=q: e.matmul(
                                cv[32 * q:32 * q + 32, :], lhsT=WS[:, j, r, :], rhs=VRS[s2][:, j, 1 + r:1 + r + NT],
                                start=(r == 0), stop=(r == 7), tile_position=(0, 32 * q)),
                                reads=['WS', 'VRa%d' % s2], writes=[cvk])
                if pend is not None:
                    stats(*pend)
                cb = cnt['cb'] % 2
                cnt['cb'] += 1
                S.op('act', lambda e, cv=cv, c=c: e.activation(
                    out=CO[:, c, 0:n], in_=cv[:, 0:n], func=AF.Identity, bias=PP[:, 4 + c:5 + c], scale=1.0),
                    reads=[cvk, 'PP'], writes=['CO%d' % c])
                S.op('act', lambda e, cv=cv, c=c, cb=cb: e.activation(
                    out=CBS[cb][:, 1, 0:n], in_=cv[:, 0:n], func=AF.Square, bias=PP[:, 4 + c:5 + c], scale=1.0),
                    reads=[cvk, 'PP'], writes=['CBSb%d' % cb])
                S.op('act', lambda e, cv=cv, c=c, cb=cb: e.activation(
                    out=CBS[cb][:, 0, 0:n], in_=cv[:, 0:n], func=AF.Identity, bias=PP[:, 4 + c:5 + c], scale=1.0),
                    reads=[cvk, 'PP'], writes=['CBSa%d' % cb])
                pend = (c, cb)
            if pm:
                stagePM(i)
            stats(*pend)

        ln_state = {}

        def stageLN_a(i):
            n = 32 if is_h(i) else NT
            cok = ['CO%d' % c for c in range(4)]
            S.op('act', lambda e: e.activation(out=MUS[:, 0:n], in_=STP[:, 0:n], func=AF.Copy),
                 reads=['STP'], writes=['MUS'])
            S.op('dve', lambda e: e.tensor_tensor(out=MSQ[:, 0:n], in0=MUS[:, 0:n], in1=MUS[:, 0:n], op=ALU.mult),
                 reads=['MUS'], writes=['MSQ'])
            S.op('dve', lambda e: e.tensor_tensor(out=MSQ[:, 0:n], in0=STP[:, n:2 * n], in1=MSQ[:, 0:n],
                                                  op=ALU.subtract), reads=['STP', 'MSQ'], writes=['MSQ'])
            tw = 32 if is_h(i) else 128
            nsub = 1 if is_h(i) else NT // 128
            for j in range(nsub):
                c0 = (i * 2 + j) * 2
                vt = LNS[:, c0:c0 + 1]
                rt = LNS[:, c0 + 1:c0 + 2]
                vk = 'LNS%d' % c0
                rk = 'LNS%d' % (c0 + 1)
                S.op('dve', lambda e, j=j, vt=vt: e.scalar_tensor_tensor(
                    out=DSC[:, 0:tw], in0=MSQ[:, j * tw:(j + 1) * tw], scalar=1.0, in1=ID32[:, 0:tw],
                    op0=ALU.mult, op1=ALU.mult, accum_out=vt), reads=['MSQ', 'ID32', 'LNS'], writes=['DSC', vk])
                S.op('dve', lambda e, vt=vt: e.tensor_scalar(out=vt, in0=vt, scalar1=EPS_LN, scalar2=None,
                                                           op0=ALU.add), reads=[vk], writes=[vk])
                S.op('pool', lambda e, vt=vt, rt=rt: e.tensor_tensor(out=rt, in0=vt, in1=NEGH[:, 0:1], op=ALU.pow),
                     reads=[vk, 'NEGH', 'LNS'], writes=[rk])

        def stageLN_b(i):
            n = 32 if is_h(i) else NT
            cok = ['CO%d' % c for c in range(4)]
            tw = 32 if is_h(i) else 128
            nsub = 1 if is_h(i) else NT // 128
            rsb, rsbk = STP[:, 0:NT], 'STP'
            for j in range(nsub):
                c0 = (i * 2 + j) * 2
                rt = LNS[:, c0 + 1:c0 + 2]
                rk = 'LNS%d' % (c0 + 1)
                S.op('pe', lambda e, j=j, rt=rt, rsb=rsb: e.transpose(
                    out=rsb[:, j * tw:(j + 1) * tw], in_=rt[0:tw, :].to_broadcast([tw, 128]),
                    identity=ID32[0:tw, 0:tw]), reads=[rk, 'ID32'], writes=[rsbk])

        def stageLN_c(i):
            n = 32 if is_h(i) else NT
            cok = ['CO%d' % c for c in range(4)]
            rsb, rsbk = STP[:, 0:NT], 'STP'
            S.op('dve', lambda e: e.tensor_tensor(
                out=CO[:, :, 0:n], in0=CO[:, :, 0:n], in1=MUS[:, 0:n].unsqueeze(1).to_broadcast([128, 4, n]),
                op=ALU.subtract), reads=cok + ['MUS'], writes=cok)
            S.op('dve', lambda e: e.tensor_tensor(
                out=CO[:, :, 0:n], in0=CO[:, :, 0:n], in1=rsb[:, 0:n].unsqueeze(1).to_broadcast([128, 4, n]),
                op=ALU.mult), reads=cok + [rsbk], writes=cok)
            for c in range(4):
                S.op('act', lambda e, c=c: e.activation(
                    out=LNO[:, c, 0:n], in_=CO[:, c, 0:n], func=AF.Silu,
                    scale=PP[:, 8 + c:9 + c], bias=PP[:, 12 + c:13 + c]),
                    reads=['CO%d' % c, 'PP'], writes=['LNO%d' % c])

        def stagePW(i):
            s2 = i % 2
            n = 32 if is_h(i) else NT
            for m2 in range(4):
                acc, acck = acc_tile()
                for c in range(4):
                    S.op('pe', lambda e, acc=acc, c=c, m2=m2: e.matmul(
                        acc[:, 0:n], lhsT=PWW[:, c, m2 * 128:(m2 + 1) * 128], rhs=LNO[:, c, 0:n],
                        start=(c == 0), stop=(c == 3)),
                        reads=['PWW', 'LNO%d' % c], writes=[acck])
                S.op('dve', lambda e, acc=acc, m2=m2: e.scalar_tensor_tensor(
                    out=MIXC[:, m2, 0:n], in0=acc[:, 0:n], scalar=PP[:, 16 + m2:17 + m2],
                    in1=SGC[s2][:, m2, 0:n], op0=ALU.add, op1=ALU.mult),
                    reads=[acck, 'PP', 'SGC%d_%d' % (s2, m2)], writes=['MIXC%d' % m2])

        out_subs = {}

        def stageOUT(i, js=None, fin=True):
            s2 = i % 2
            s3 = i % 3
            g = i - 1
            tw = 32 if is_h(i) else 128
            subs = out_subs.setdefault(i, [])
            for j in (range(1 if is_h(i) else 2) if js is None else js):
                xk = 'X%dj%d' % (s3, j)
                for nh in range(2):
                    o, ok = o_tile()
                    for e_ in range(8):
                        lh = MIXP[s2][:, e_, j * tw:(j + 1) * tw] if e_ < 4 else MIXC[:, e_ - 4, j * tw:(j + 1) * tw]
                        lk = ('MIXP%d_%d' % (s2, e_)) if e_ < 4 else ('MIXC%d' % (e_ - 4))
                        S.op('pe', lambda e, o=o, lh=lh, e_=e_, nh=nh: e.matmul(
                            o[0:tw, :], lhsT=lh, rhs=WOUT[:, e_, nh * 512:(nh + 1) * 512],
                            start=(e_ == 0), stop=(e_ == 7)), reads=[lk, 'WOUTh%d' % (e_ // 4)], writes=[ok])
                    S.op('dve', lambda e, o=o, j=j, nh=nh: e.tensor_tensor(
                        out=XB[s3][0:tw, j, nh * 512:(nh + 1) * 512], in0=o[0:tw, :],
                        in1=XB[s3][0:tw, j, nh * 512:(nh + 1) * 512], op=ALU.add),
                        reads=[ok, xk], writes=[xk])
                subs.append((j, xk, statc(i, j, 2), statc(i, j, 3), XB[s3][0:tw, j, :]))
            if not fin:
                return
            for j, xk, (ss, ssk), (rs, rsk), yt in subs:
                jt, jk = junk_tile()
                S.op('act', lambda e, yt=yt, ss=ss, jt=jt: e.activation(
                    out=jt[0:tw, :], in_=yt, func=AF.Square, scale=1.0 / 32.0, accum_out=ss[0:tw, :]),
                    reads=[xk, 'STAT'], writes=[ssk, jk])
            for j, xk, (ss, ssk), (rs, rsk), yt in subs:
                S.op('dve', lambda e, ss=ss: e.tensor_scalar(
                    out=ss[0:tw, :], in0=ss[0:tw, :], scalar1=EPS_RMS, scalar2=None, op0=ALU.add),
                    reads=[ssk], writes=[ssk])
            for j, xk, (ss, ssk), (rs, rsk), yt in subs:
                S.op('pool', lambda e, ss=ss, rs=rs: e.tensor_tensor(
                    out=rs[0:tw, :], in0=ss[0:tw, :], in1=NEGH[0:tw, 0:1], op=ALU.pow),
                    reads=[ssk, 'NEGH', 'STAT'], writes=[rsk])
            for j, xk, (ss, ssk), (rs, rsk), yt in subs:
                S.op('dve', lambda e, yt=yt, rs=rs: e.scalar_tensor_tensor(
                    out=yt, in0=yt, scalar=rs[0:tw, :], in1=FGB[0:tw, :], op0=ALU.mult, op1=ALU.mult),
                    reads=[xk, rsk, 'FGB'], writes=[xk])
                if is_h(i):
                    S.op('sp', lambda e, yt=yt: e.dma_start(out=ys_d, in_=yt), reads=[xk], writes=['o_ys'],
                         dma='st_s')
                else:
                    dst = yp_d[g * NT + j * 128:g * NT + (j + 1) * 128, :]
                    S.op('sp', lambda e, yt=yt, dst=dst: e.dma_start(out=dst, in_=yt), reads=[xk],
                         writes=['o_yp%d_%d' % (i, j)], dma='st%d' % ((i * 2 + j) % 4))

        def state_rows(src, nrows, scale, dst, stg, stgk, reads, okey, chan):
            o, ok = o_tile()
            for c in range(4):
                S.op('pe', lambda e, c=c: e.transpose(
                    out=o[0:nrows, c * 128:(c + 1) * 128], in_=src(c), identity=ID32[:]),
                    reads=reads(c) + ['ID32'], writes=[ok])
            S.op('act', lambda e: e.activation(out=stg[0:nrows, :], in_=o[0:nrows, :], func=AF.Copy, scale=scale),
                 reads=[ok], writes=[stgk])
            S.op('sp', lambda e: e.dma_start(out=dst, in_=stg[0:nrows, :]), reads=[stgk], writes=[okey], dma=chan)

        def state_out(kind):
            if kind == 'ss_pool':
                for s in range(2):
                    state_rows(lambda c, s=s: US[:, c, s, 17:32], 15, 1.0, ss_pool_d[s * 15:(s + 1) * 15, :],
                               (STA, STB)[s], ('STA', 'STB')[s], lambda c: ['USn%d' % c], 'o_ssp%d' % s,
                               ('st_a', 'st_b')[s])
            elif kind == 'ss_conv':
                for s in range(2):
                    state_rows(lambda c, s=s: VS32[:, c, s, 18:48], 30, 0.5, ss_conv_d[s * 30:(s + 1) * 30, :],
                               (STA, STB)[s], ('STA', 'STB')[s], lambda c: ['VS32n%d' % c, 'VS32h'],
                               'o_ssc%d' % s, ('st_a', 'st_b')[s])
            elif kind == 'sp_pool':
                s2 = (NPOS - 1) % 2
                state_rows(lambda c: U[s2][:, c, 16 + NT - 15:16 + NT], 15, 1.0, sp_pool_d, STA, 'STA',
                           lambda c: ['U%dc%d' % (s2, c)], 'o_spp', 'st_a')
            else:
                state_rows(lambda c: V32T[:, c, 2:32], 30, 0.5, sp_conv_d, STB, 'STB',
                           lambda c: ['V32T%d' % c], 'o_spc', 'st_b')

        load_x(1)
        stageI_pre(0)
        stageI_pre(1)
        load_x(2)
        cache_prep()
        stageI_pe(0)
        stageI_pe(1)
        stageI_pre(2)
        load_misc_weights_a()
        build_ws()
        for sg in range(3):
            stageP(0, segs=(sg,), tail=(sg == 1))
            if sg == 1:
                replicate_v(0)
            if sg == 2:
                stagePOOL(0)
            stageP(1, rep=False, segs=(sg,), tail=(sg == 2))
            if sg == 1:
                replicate_v(1)
            if sg == 2:
                stagePOOL(1)
        stageP(0, segs=(3,), tail=False)
        stageP(1, rep=False, segs=(3,), tail=False)
        stageP(0, segs=(4,), tail=False)
        stageP(1, rep=False, segs=(4,), tail=False)
        stageI_pe(2)
        stageCONV(0)
        stageLN_a(0)
        load_wout()
        stageP(2, rep=False, segs=(0,), tail=False)
        stageP(2, rep=True, segs=(1,), tail=False)
        stageLN_b(0)
        stageLN_c(0)
        for i in range(NPOS):
            if i + 2 < NPOS and i + 2 > 2:
                load_x(i + 2)
            if i == 1:
                stageP(2, rep=False, segs=(2,), tail=False)
            elif i + 1 < NPOS and i >= 1:
                stageP(i + 1, segs=(0, 1, 2), tail=False)
            if i >= 1 and i != NPOS - 1:
                stageLN_c(i)
            if i + 1 < NPOS and i >= 1:
                stageP(i + 1, segs=(3, 4), tail=True)
                stagePOOL(i + 1)
            if i != NPOS - 1:
                stagePW(i)
            if i == NPOS - 2:
                state_out('sp_pool')
                state_out('sp_conv')
            if i + 2 < NPOS and i + 2 > 2:
                stageI_pre(i + 2)
            if i + 1 < NPOS:
                stageCONV(i + 1)
                stageLN_a(i + 1)
            if i + 2 < NPOS and i + 2 > 2:
                stageI_pe(i + 2)
            if i == NPOS - 2:
                stageOUT(i, js=(0,), fin=False)
                stageLN_b(i + 1)
                stageLN_c(i + 1)
                stageOUT(i, js=(1,), fin=False)
                stagePW(i + 1)
                stageOUT(i, js=(), fin=True)
            else:
                stageOUT(i)
                if i + 1 < NPOS:
                    stageLN_b(i + 1)
            if i == 0:
                state_out('ss_pool')
                state_out('ss_conv')
        outs = ['o_ys', 'o_ssp0', 'o_ssp1', 'o_ssc0', 'o_ssc1', 'o_spp', 'o_spc']
        outs += ['o_yp%d_%d' % (i, j) for i in range(1, NPOS) for j in range(2)]
        S.op('sp', None, reads=outs)
        info = S.emit(st)
    return nc, info


_CACHE = {}


def _get_program():
    if 'nc' not in _CACHE:
        _CACHE['nc'], _CACHE['info'] = build_program()
    return _CACHE['nc']


def _pack_pp(inp, pos0):
    pp = np.zeros((128, NPP), np.float32)

    def col4(v):
        return np.ascontiguousarray(np.asarray(v, np.float32).reshape(4, 128).T)
    pp[:, 0:4] = col4(inp['pool_scale'][0])
    pp[:, 4:8] = col4(inp['dw_b'][0])
    pp[:, 8:12] = col4(inp['ln_g'][0])
    pp[:, 12:16] = col4(inp['ln_b'][0])
    pp[:, 16:20] = col4(inp['pw_b'][0])
    dw = np.asarray(inp['dw_w'][0], np.float32)
    pp[:, 20:144] = dw.T.reshape(4, 128, NTAP).transpose(1, 0, 2).reshape(128, 4 * NTAP)
    pp[:, 144:152] = np.asarray(inp['norm_g'][0], np.float32).reshape(8, 128).T
    pp[:, 152] = pos0
    dwp = np.concatenate([np.zeros((1, 512), np.float32), dw], axis=0)
    pp[:, 153:281] = dwp.reshape(4, 8, 16, 32).transpose(0, 3, 2, 1).reshape(128, 128)
    pp[:, 281:313] = np.tile(np.eye(32, dtype=np.float32), (4, 1))
    return pp


def kernel(**inputs):
    inp = {k: np.asarray(v) for k, v in inputs.items()}
    xpr = np.asarray(inp['x_prompt'], np.float32)
    xs = np.asarray(inp['x_sample'], np.float32)
    cpool = np.asarray(inp['cache_pool'], np.float32)[0]
    cconv = np.asarray(inp['cache_conv'], np.float32)[0]
    nc = _get_program()
    shared = dict(
        w_in=np.ascontiguousarray(inp['w_in'][0], np.float32),
        w_out=np.ascontiguousarray(inp['w_out'][0], np.float32),
        pw_w=np.ascontiguousarray(inp['pw_w'][0], np.float32),
        pool_mix=np.ascontiguousarray(np.asarray(inp['pool_mix'][0], np.float32).reshape(512, 128)),
        final_g=np.ascontiguousarray(np.asarray(inp['final_g'], np.float32).reshape(1, 1024)),
        norm_g=np.ascontiguousarray(np.asarray(inp['norm_g'], np.float32).reshape(1, 1024)),
        ident=np.eye(128, dtype=np.float32),
    )
    in_maps = []
    for c in range(8):
        b, s = c // 2, c % 2
        xp = np.ascontiguousarray(xpr[b, s * 2048:(s + 1) * 2048])
        xh = np.zeros((64, 1024), np.float32)
        xh[0:32] = xs[2 * c:2 * c + 2].reshape(32, 1024)
        if s == 1:
            xh[32:64] = xpr[b, 2048 - 32:2048]
        m = dict(shared)
        m.update(xp=xp, xh=xh,
                 cpool=np.ascontiguousarray(cpool[2 * c:2 * c + 2].reshape(30, 512)),
                 cconv=np.ascontiguousarray(cconv[2 * c:2 * c + 2].reshape(60, 512)),
                 pp=_pack_pp(inp, float(s * 2048)))
        in_maps.append(m)
    res = run_bass_kernel_spmd(nc, in_maps, core_ids=list(range(8)))
    r = res.results
    y_prompt = np.empty((4, 4096, 1024), np.float32)
    y_sample = np.empty((16, 16, 1024), np.float32)
    sp_pool = np.empty((1, 4, 15, 512), np.float32)
    sp_conv = np.empty((1, 4, 30, 512), np.float32)
    ss_pool = np.empty((1, 16, 15, 512), np.float32)
    ss_conv = np.empty((1, 16, 30, 512), np.float32)
    for c in range(8):
        b, s = c // 2, c % 2
        y_prompt[b, s * 2048:(s + 1) * 2048] = r[c]['yp']
        y_sample[2 * c:2 * c + 2] = np.asarray(r[c]['ys']).reshape(2, 16, 1024)
        ss_pool[0, 2 * c:2 * c + 2] = np.asarray(r[c]['ss_pool']).reshape(2, 15, 512)
        ss_conv[0, 2 * c:2 * c + 2] = np.asarray(r[c]['ss_conv']).reshape(2, 30, 512)
        if s == 1:
            sp_pool[0, b] = r[c]['sp_pool']
            sp_conv[0, b] = r[c]['sp_conv']
    return (y_prompt, y_sample, sp_pool, sp_conv, ss_pool, ss_conv)
```

```python
from contextlib import ExitStack

import numpy as np
import concourse.bass as bass
import concourse.mybir as mybir
from concourse.bass_utils import run_bass_kernel_spmd

F32 = mybir.dt.float32
BF16 = mybir.dt.bfloat16
ALU = mybir.AluOpType
AF = mybir.ActivationFunctionType

NT = 256
NPOS = 9
WINS = (2, 4, 8, 16)
NTAP = 31
EPS_RMS = 1e-6
EPS_LN = 1e-5
NPP = 313


class Sched:
    def __init__(self, nc):
        self.nc = nc
        self.ops = []
        self.last_w = {}
        self.readers = {}
        self.last_dma = {}
        self.cur_batch = {}
        self.batch_last = {}

    def op(self, eng, fn, reads=(), writes=(), dma=None, batch=None):
        idx = len(self.ops)
        deps = set()
        for k in reads:
            if k in self.last_w:
                deps.add(self.last_w[k])
        for k in writes:
            if k in self.last_w:
                deps.add(self.last_w[k])
            deps.update(self.readers.get(k, ()))
        same_batch = (batch is not None and self.cur_batch.get(dma) == batch)
        if dma is not None and dma in self.last_dma and not same_batch:
            deps.add(self.last_dma[dma])
        deps.discard(idx)
        self.ops.append(dict(eng=eng, fn=fn, deps=deps, dma=dma, batch=batch))
        if batch is not None:
            self.cur_batch[dma] = batch
            self.batch_last[(dma, batch)] = idx
        if dma is not None:
            self.last_dma[dma] = idx
        for k in writes:
            self.last_w[k] = idx
            self.readers[k] = set()
        for k in reads:
            self.readers.setdefault(k, set()).add(idx)
        return idx

    def emit(self, stack):
        nc = self.nc
        engs = {'pe': nc.tensor, 'act': nc.scalar, 'dve': nc.vector,
                'pool': nc.gpsimd, 'sp': nc.sync}
        ops = self.ops
        pos = {}
        cnt = {}
        for i, o in enumerate(ops):
            st = ('dma', o['dma']) if o['dma'] is not None else o['eng']
            o['stream'] = st
            cnt[st] = cnt.get(st, 0) + 1
            pos[i] = cnt[st]
        waited = {}
        signal = set()
        for i, o in enumerate(ops):
            need = {}
            for d in o['deps']:
                if ops[d]['batch'] is not None:
                    if o['batch'] == ops[d]['batch'] and o['dma'] == ops[d]['dma']:
                        continue
                    d = self.batch_last[(ops[d]['dma'], ops[d]['batch'])]
                sd = ops[d]['stream']
                if sd == 'pe' and o['eng'] == 'pe' and o['dma'] is None:
                    continue
                if sd not in need or pos[d] > pos[need[sd]]:
                    need[sd] = d
            w = []
            for sd, d in need.items():
                key = (o['eng'], sd)
                if waited.get(key, 0) < pos[d]:
                    waited[key] = pos[d]
                    w.append(d)
                    signal.add(d)
            o['waits'] = w
        val = {}
        run = {}
        for i, o in enumerate(ops):
            st = o['stream']
            if isinstance(st, tuple):
                run[st] = run.get(st, 0) + 16
                val[i] = run[st]
            elif i in signal:
                run[st] = run.get(st, 0) + 1
                val[i] = run[st]
        sems = {}
        for st in cnt:
            nm = ('d_' + st[1]) if isinstance(st, tuple) else ('e_' + st)
            sems[st] = stack.enter_context(nc.semaphore(nm))
        nwait = 0
        for i, o in enumerate(ops):
            e = engs[o['eng']]
            ws = list(o['waits'])
            attach = ws.pop() if (ws and o['fn'] is not None) else None
            for d in ws:
                e.wait_ge(sems[ops[d]['stream']], val[d])
                nwait += 1
            if o['fn'] is None:
                continue
            ins = o['fn'](e)
            if attach is not None:
                ins._wait_ge(sems[ops[attach]['stream']], val[attach])
            st = o['stream']
            if isinstance(st, tuple):
                ins.then_inc(sems[st], 16)
            elif i in signal:
                ins.then_inc(sems[st], 1)
        return dict(n_ops=len(ops), n_waits=nwait, n_signal=len(signal))


def build_program():
    nc = bass.Bass("TRN2", target_bir_lowering=False)

    def din(name, shape):
        return nc.dram_tensor(name, list(shape), F32, kind="ExternalInput").ap()

    def dout(name, shape):
        return nc.dram_tensor(name, list(shape), F32, kind="ExternalOutput").ap()

    xp_d = din("xp", (2048, 1024))
    xh_d = din("xh", (64, 1024))
    cpool_d = din("cpool", (30, 512))
    cconv_d = din("cconv", (60, 512))
    win_d = din("w_in", (1024, 2560))
    wout_d = din("w_out", (1024, 1024))
    pww_d = din("pw_w", (512, 512))
    pmw_d = din("pool_mix", (512, 128))
    fg_d = din("final_g", (1, 1024))
    ng_d = din("norm_g", (1, 1024))
    pp_d = din("pp", (128, NPP))
    id_d = din("ident", (128, 128))
    yp_d = dout("yp", (2048, 1024))
    ys_d = dout("ys", (32, 1024))
    sp_pool_d = dout("sp_pool", (15, 512))
    sp_conv_d = dout("sp_conv", (30, 512))
    ss_pool_d = dout("ss_pool", (30, 512))
    ss_conv_d = dout("ss_conv", (60, 512))

    with ExitStack() as st:
        def sb(n, s, d):
            return st.enter_context(nc.sbuf_tensor(n, list(s), d))

        def ps(n, s, d):
            return st.enter_context(nc.psum_tensor(n, list(s), d))

        WIN = sb("WIN", (128, 8, 2560), BF16)
        WOUT = sb("WOUT", (128, 8, 1024), BF16)
        PWW = sb("PWW", (128, 4, 512), BF16)
        PMW = sb("PMW", (128, 4, 128), BF16)
        WS = sb("WS", (128, 16, 8, 32), BF16)
        VRS = [sb("VR%d" % s_, (128, 16, NT + 8), BF16) for s_ in range(2)]
        VRH = sb("VRH", (128, 4, 4, 2, 48), BF16)
        GB = sb("GB", (128, 1024), F32)
        ONES = sb("ONES", (128, 128), BF16)
        ID32 = sb("ID32", (128, 128), F32)
        IDB = sb("IDB", (128, 128), BF16)
        FGB = sb("FGB", (128, 1024), F32)
        PP = sb("PP", (128, NPP), F32)
        DWS = sb("DWS", (128, 128), F32)
        NEGH = sb("NEGH", (128, 8), F32)
        STAT = sb("STAT", (128, 80), F32)
        IOT = sb("IOT", (128, 16), F32)
        CNT = sb("CNT", (128, 16), F32)
        INV = sb("INV", (128, 4, 16), F32)
        T16 = sb("T16", (128, 16), F32)
        XB = [sb("XB%d" % s, (128, 2, 1024), F32) for s in range(3)]
        JUNKS = [sb("JUNK%d" % j, (128, 1024), BF16) for j in range(1)]
        HG = [sb("HG%d" % h, (128, 1024), BF16) for h in range(3)]
        HT = sb("HT", (128, 8, NT), BF16)
        HTH = sb("HTH", (128, 8, 64), BF16)
        THH = sb("THH", (128, 4, 64), F32)
        U = [sb("U%d" % s, (128, 4, 16 + NT), F32) for s in range(2)]
        V = [sb("V%d" % s, (128, 4, 32 + NT), BF16) for s in range(2)]
        US = sb("US", (128, 4, 2, 32), F32)
        VS32 = sb("VS32", (128, 4, 2, 48), F32)
        VSB = sb("VSB", (128, 4, 2, 48), BF16)
        V32T = sb("V32T", (128, 4, 32), F32)
        SGP = [sb("SGP%d" % s, (128, 4, NT), F32) for s in range(2)]
        SGC = [sb("SGC%d" % s, (128, 4, NT), F32) for s in range(2)]
        TH = [sb("TH%d" % s, (128, NT), F32) for s in range(4)]
        PA = sb("PA", (128, 16 + NT), F32)
        PB = sb("PB", (128, 16 + NT), F32)
        PLS = [sb("PL%d" % s, (128, 4, NT), BF16) for s in range(2)]
        CO = sb("CO", (128, 4, NT), F32)
        CBS = [sb("CBS%d" % s, (128, 2, NT), BF16) for s in range(2)]
        MUS = sb("MUS", (128, NT), F32)
        MSQ = sb("MSQ", (128, NT), F32)
        LNS = sb("LNS", (128, 64), F32)
        DSC = sb("DSC", (128, 128), F32)
        LNO = sb("LNO", (128, 4, NT), BF16)
        MIXP = [sb("MIXP%d" % s, (128, 4, NT), BF16) for s in range(2)]
        MIXC = sb("MIXC", (128, 4, NT), BF16)
        STA = sb("STA", (64, 512), F32)
        STB = sb("STB", (64, 512), F32)
        GEN = [ps("GEN%d" % s, (128, 512), F32) for s in range(5)]
        GENB = [g.bitcast(BF16) for g in GEN]
        STP = ps("STP", (128, 512), F32)
        OB = [ps("OB%d" % s, (128, 512), F32) for s in range(2)]

        S = Sched(nc)
        cnt = dict(acc=0, cv=0, o=0, tp=0, hg=0, th=0, stg=0, cb=0, junk=0)

        def junk_tile():
            return JUNKS[0], 'JUNK0'

        def acc_tile():
            i = cnt['acc'] % 5
            cnt['acc'] += 1
            return GEN[i][:, 0:NT], "GEN%d" % i

        def tp_tile():
            i = cnt['acc'] % 5
            cnt['acc'] += 1
            return GENB[i], "GEN%d" % i

        cv_tile = acc_tile

        def o_tile():
            i = cnt['o'] % 2
            cnt['o'] += 1
            return OB[i], "O%d" % i

        def statc(i, j, kind):
            c = (i * 2 + j) * 4 + kind
            return STAT[:, c:c + 1], "ST%d" % c

        def r3(ap, a):
            return ap.rearrange("p (a b) -> p a b", a=a)

        def wload(dst, src, key, chan):
            S.op('pool', lambda e: e.dma_start(out=dst, in_=src), writes=[key], dma=chan)

        def load_win():
            wv = win_d.rearrange("(k p) e -> p k e", p=128)
            n_ = 0
            for seg in (3, 2, 0, 1, 4):
                for hf in range(2):
                    lo = seg * 512 + hf * 256
                    wload(WIN[:, :, lo:lo + 256], wv[:, :, lo:lo + 256], 'WINs%d_%d' % (seg, hf), 'w%d' % n_)
                    n_ += 1

        def load(dst, src, key, chan):
            S.op('sp', lambda e: e.dma_start(out=dst, in_=src), writes=[key], dma=chan)

        load_win()
        load(PP[:], pp_d, 'PP', 'c_pp')
        load(ID32[:], id_d, 'ID32', 'c_id')
        load(XB[0][0:64, 0, :], xh_d, 'X0j0', 'ldx0')
        load(STA[0:30, :], cpool_d, 'STA', 'c_sta')
        load(STB[0:60, :], cconv_d, 'STB', 'c_stb')
        S.op('dve', lambda e: e.memset(STAT[:], 0.0), writes=['STAT'])
        S.op('dve', lambda e: e.memset(NEGH[:], -0.5), writes=['NEGH'])
        S.op('dve', lambda e: e.memset(LNS[:], 0.0), writes=['LNS'])
        S.op('dve', lambda e: e.memset(VS32[:], 0.0), writes=['VS32h'] + ['VS32n%d' % c for c in range(4)] + ['VS32h_%d' % c for c in range(3)])
        S.op('dve', lambda e: e.memset(US[:], 0.0), writes=['USn%d' % c for c in range(4)] + ['USh%d' % c for c in range(4)])
        S.op('dve', lambda e: e.memset(ONES[:], 1.0 / 512.0), writes=['ONES'])
        S.op('pool', lambda e: e.iota(IOT[:], [[1, 16]], base=0, channel_multiplier=0,
                                      allow_small_or_imprecise_dtypes=True), writes=['IOT'])
        S.op('dve', lambda e: e.tensor_copy(out=IDB[:], in_=ID32[:]), reads=['ID32'], writes=['IDB'])
        S.op('dve', lambda e: e.tensor_scalar(out=DWS[:], in0=PP[:, 153:281], scalar1=0.5, scalar2=None,
                                              op0=ALU.mult), reads=['PP'], writes=['DWS'])
        load(GB[:], ng_d.partition_broadcast(128), 'GB', 'c_gb')
        def build_inv():
            S.op('dve', lambda e: e.tensor_scalar(out=CNT[:], in0=IOT[:], scalar1=PP[:, 152:153], scalar2=1.0,
                                                  op0=ALU.add, op1=ALU.add), reads=['IOT', 'PP'], writes=['CNT'])
            for m, w in enumerate(WINS):
                S.op('dve', lambda e, m=m, w=w: e.tensor_scalar(out=INV[:, m, :], in0=CNT[:], scalar1=float(w),
                                                                scalar2=None, op0=ALU.min),
                     reads=['CNT'], writes=['INV%d' % m])
                S.op('dve', lambda e, m=m: e.reciprocal(out=INV[:, m, :], in_=INV[:, m, :]),
                     reads=['INV%d' % m], writes=['INV%d' % m])

        PROJ_ORDER = [12, 13, 14, 15, 8, 9, 10, 11, 0, 1, 2, 3, 4, 5, 6, 7, 16, 17, 18, 19]

        def load_misc_weights_a():
            wload(PMW[:], pmw_d.rearrange("(g p) d -> p g d", p=128), 'PMW', 'w0')
            wload(PWW[:], pww_d.rearrange("(c p) e -> p c e", p=128), 'PWW', 'w1')

        def load_wout():
            wo = wout_d.rearrange("(k p) e -> p k e", p=128)
            for h in range(2):
                wload(WOUT[:, 4 * h:4 * h + 4, :], wo[:, 4 * h:4 * h + 4, :], 'WOUTh%d' % h, 'w%d' % (2 - h))
            load(FGB[:], fg_d.partition_broadcast(128), 'FGB', 'c_fg')

        def build_ws():
            S.op('dve', lambda e: e.tensor_tensor(
                out=WS[:].rearrange("p j r m -> p (j r) m"),
                in0=PP[:, 281:313].unsqueeze(1).to_broadcast([128, 128, 32]),
                in1=DWS[:].unsqueeze(2).to_broadcast([128, 128, 32]), op=ALU.mult),
                reads=['PP', 'DWS'], writes=['WS'])

        def replicate_v(i):
            s2 = i % 2
            for sh in range(4):
                for q in range(4):
                    if is_h(i):
                        src = VSB[32 * q:32 * q + 32, :, :, 8 * sh:8 * sh + 24].rearrange("p c s x -> p (c s) x")
                        dst = VRH[32 * sh:32 * sh + 32, q, :, :, 0:24].rearrange("p c s x -> p (c s) x")
                        rk, wk, ch = ['VSB'], 'VRH', 'vrh'
                    else:
                        src = V[s2][32 * q:32 * q + 32, :, 8 * sh:8 * sh + NT + 8]
                        dst = VRS[s2][32 * sh:32 * sh + 32, :, :].rearrange("p (c q) x -> p c q x", q=4)[:, :, q, :]
                        rk = ['V%dc%d' % (s2, c) for c in range(4)] + ['V%dm' % s2]
                        wk, ch = 'VRa%d' % s2, 'vrp'
                    S.op('pool', lambda e, src=src, dst=dst: e.dma_start(out=dst, in_=src),
                         reads=rk, writes=[wk], dma=ch, batch=i)

        def is_h(i):
            return i == 0

        def load_x(i):
            g = i - 1
            s = i % 3
            src = xp_d[g * NT:(g + 1) * NT, :].rearrange("(j p) d -> p j d", p=128)
            if i == 1:
                for j, eng in ((0, 'sp'), (1, 'act')):
                    S.op(eng, lambda e, j=j: e.dma_start(out=XB[s][:, j, :], in_=src[:, j, :]),
                         writes=['X%dj%d' % (s, j)], dma='ldx%d_%d' % (s, j))
                return
            S.op('sp', lambda e: e.dma_start(out=XB[s][:], in_=src),
                 writes=['X%dj0' % s, 'X%dj1' % s], dma='ldx%d' % s)

        hg_of = {}

        def stageI_pre(i):
            s = i % 3
            npart = 64 if is_h(i) else 128
            subs = []
            for j in range(1 if is_h(i) else 2):
                h = cnt['hg'] % 3
                cnt['hg'] += 1
                hg_of[(i, j)] = h
                subs.append((j, XB[s][0:npart, j, :], 'X%dj%d' % (s, j), statc(i, j, 0), statc(i, j, 1), h))
            for j, xt, xk, (ss, ssk), (rs, rsk), h in subs:
                jt, jk = junk_tile()
                S.op('act', lambda e, xt=xt, ss=ss, jt=jt: e.activation(
                    out=jt[0:npart, :], in_=xt, func=AF.Square, scale=1.0 / 32.0, accum_out=ss[0:npart, :]),
                    reads=[xk, 'STAT'], writes=[ssk, jk])
            for j, xt, xk, (ss, ssk), (rs, rsk), h in subs:
                S.op('dve', lambda e, ss=ss: e.tensor_scalar(
                    out=ss[0:npart, :], in0=ss[0:npart, :], scalar1=EPS_RMS, scalar2=None, op0=ALU.add),
                    reads=[ssk], writes=[ssk])
            for j, xt, xk, (ss, ssk), (rs, rsk), h in subs:
                S.op('pool', lambda e, ss=ss, rs=rs: e.tensor_tensor(
                    out=rs[0:npart, :], in0=ss[0:npart, :], in1=NEGH[0:npart, 0:1], op=ALU.pow),
                    reads=[ssk, 'NEGH', 'STAT'], writes=[rsk])
            for j, xt, xk, (ss, ssk), (rs, rsk), h in subs:
                S.op('dve', lambda e, xt=xt, rs=rs, h=h: e.scalar_tensor_tensor(
                    out=HG[h][0:npart, :], in0=xt, scalar=rs[0:npart, :], in1=GB[0:npart, :],
                    op0=ALU.mult, op1=ALU.mult), reads=[xk, rsk, 'GB'], writes=['HG%d' % h])

        def stageI_pe(i):
            nsub = 1 if is_h(i) else 2
            tw = 64 if is_h(i) else 128
            for j in range(nsub):
                h = hg_of[(i, j)]
                for half in range(1 if is_h(i) else 2):
                    nk = 8 if is_h(i) else 4
                    i_ = cnt['acc'] % 5
                    cnt['acc'] += 1
                    tp = GEN[i_]
                    tpk = "GEN%d" % i_
                    for kk in range(nk):
                        k = half * nk + kk
                        S.op('pe', lambda e, k=k, kk=kk, h=h, tp=tp: e.matmul(
                            tp[:, kk * tw:(kk + 1) * tw], lhsT=HG[h][0:tw, k * 128:(k + 1) * 128],
                            rhs=IDB[0:tw, 0:tw], start=True, stop=True),
                            reads=['HG%d' % h, 'IDB'], writes=[tpk])
                    hts = HTH if is_h(i) else HT
                    S.op('act', lambda e, tp=tp, j=j, half=half, nk=nk, hts=hts: e.activation(
                        out=hts[:, half * nk:(half + 1) * nk, j * tw:(j + 1) * tw], in_=r3(tp[:, 0:nk * tw], nk),
                        func=AF.Copy), reads=[tpk], writes=['HTH'] if is_h(i) else ['HTj%d_%d' % (j, half)])

        th_of = {}

        def stageP(i, rep=True, segs=(0, 1, 2, 3, 4), tail=True):
            n = 64 if is_h(i) else NT
            s2 = i % 2
            sn = (i + 1) % 2
            htk = ['HTH'] if is_h(i) else ['HTj0_0', 'HTj0_1', 'HTj1_0', 'HTj1_1']
            hts = HTH if is_h(i) else HT
            last = (i == NPOS - 1)
            th_cur = th_of.setdefault(i, {})
            if tuple(segs) == (0, 1, 2):
                order = [12, 8, 13, 9, 14, 10, 15, 11, 0, 1, 2, 3]
            elif len(segs) == 5:
                order = [12, 8, 13, 9, 14, 10, 15, 11] + PROJ_ORDER[8:]
            else:
                order = [m_ for sg in segs for m_ in PROJ_ORDER[4 * sg:4 * sg + 4]]
            for m in order:
                acc, acck = acc_tile()
                for k in range(8):
                    S.op('pe', lambda e, k=k, m=m, acc=acc: e.matmul(
                        acc[:, 0:n], lhsT=WIN[:, k, m * 128:(m + 1) * 128], rhs=hts[:, k, 0:n],
                        start=(k == 0), stop=(k == 7)), reads=htk + ['WINs%d_%d' % (m // 4, (m % 4) // 2)], writes=[acck])
                if 12 <= m < 16:
                    c = m - 12
                    if is_h(i):
                        t = 'H%d' % c
                        tht = THH[:, c, :]
                    else:
                        t = cnt['th'] % 4
                        cnt['th'] += 1
                        tht = TH[t]
                    th_cur[c] = (t, tht)
                    S.op('act', lambda e, acc=acc, tht=tht: e.activation(
                        out=tht[:, 0:n], in_=acc[:, 0:n], func=AF.Tanh, scale=0.5),
                        reads=[acck], writes=['TH%s' % t])
                elif 8 <= m < 12:
                    c = m - 8
                    t, tht = th_cur[c]
                    if is_h(i):
                        S.op('dve', lambda e, acc=acc, tht=tht, c=c: e.scalar_tensor_tensor(
                            out=VS32[:, c, :, 32:48], in0=r3(tht[:, 0:32], 2), scalar=1.0,
                            in1=r3(acc[:, 0:32], 2), op0=ALU.add, op1=ALU.mult),
                            reads=[acck, 'TH%s' % t], writes=['VS32n%d' % c])
                        S.op('dve', lambda e, acc=acc, tht=tht, c=c: e.scalar_tensor_tensor(
                            out=V[sn][:, c, 0:32], in0=tht[:, 32:64], scalar=1.0,
                            in1=acc[:, 32:64], op0=ALU.add, op1=ALU.mult),
                            reads=[acck, 'TH%s' % t], writes=['V%dm' % sn])
                    else:
                        S.op('dve', lambda e, acc=acc, tht=tht, c=c: e.scalar_tensor_tensor(
                            out=V[s2][:, c, 32:32 + NT], in0=tht[:], scalar=1.0,
                            in1=acc, op0=ALU.add, op1=ALU.mult),
                            reads=[acck, 'TH%s' % t], writes=['V%dc%d' % (s2, c)])
                        if last:
                            S.op('dve', lambda e, acc=acc, tht=tht, c=c: e.scalar_tensor_tensor(
                                out=V32T[:, c, :], in0=tht[:, NT - 32:NT], scalar=1.0,
                                in1=acc[:, NT - 32:NT], op0=ALU.add, op1=ALU.mult),
                                reads=[acck, 'TH%s' % t], writes=['V32T%d' % c])
                    if m == 11 and rep and not is_h(i):
                        replicate_v(i)
                elif m < 4:
                    if is_h(i):
                        S.op('act', lambda e, acc=acc, m=m: e.activation(
                            out=US[:, m, :, 16:32], in_=r3(acc[:, 0:32], 2), func=AF.Copy),
                            reads=[acck], writes=['USn%d' % m])
                        S.op('act', lambda e, acc=acc, m=m: e.activation(
                            out=U[sn][:, m, 0:16], in_=acc[:, 48:64], func=AF.Copy),
                            reads=[acck], writes=['U%dm' % sn])
                    else:
                        S.op('act', lambda e, acc=acc, m=m: e.activation(
                            out=U[s2][:, m, 16:16 + NT], in_=acc, func=AF.Copy),
                            reads=[acck], writes=['U%dc%d' % (s2, m)])
                elif m < 8:
                    mm_ = m - 4
                    nn = 32 if is_h(i) else NT
                    S.op('act', lambda e, acc=acc, mm_=mm_, nn=nn: e.activation(
                        out=SGP[s2][:, mm_, 0:nn], in_=acc[:, 0:nn], func=AF.Silu),
                        reads=[acck], writes=['SGP%d_%d' % (s2, mm_)])
                else:
                    mm_ = m - 16
                    nn = 32 if is_h(i) else NT
                    S.op('act', lambda e, acc=acc, mm_=mm_, nn=nn: e.activation(
                        out=SGC[s2][:, mm_, 0:nn], in_=acc[:, 0:nn], func=AF.Silu),
                        reads=[acck], writes=['SGC%d_%d' % (s2, mm_)])
            if not tail:
                return
            if is_h(i):
                S.op('dve', lambda e: e.tensor_copy(out=VSB[:], in_=VS32[:]),
                     reads=['VS32n%d' % c for c in range(4)] + ['VS32h'], writes=['VSB'])
            if is_h(i):
                pass
            elif not last:
                S.op('pool', lambda e: e.tensor_copy(out=V[sn][:, :, 0:32], in_=V[s2][:, :, NT:NT + 32]),
                     reads=['V%dc%d' % (s2, c) for c in range(4)], writes=['V%dm' % sn])
                S.op('pool', lambda e: e.tensor_copy(out=U[sn][:, :, 0:16], in_=U[s2][:, :, NT:NT + 16]),
                     reads=['U%dc%d' % (s2, c) for c in range(4)], writes=['U%dm' % sn])

        def cache_prep():
            for m in range(4):
                acc, acck = acc_tile()
                S.op('pe', lambda e, acc=acc, m=m: e.transpose(
                    out=acc[:, 0:30], in_=STA[0:30, m * 128:(m + 1) * 128], identity=ID32[0:30, 0:30]),
                    reads=['STA', 'ID32'], writes=[acck])
                S.op('act', lambda e, acc=acc, m=m: e.activation(
                    out=US[:, m, :, 1:16], in_=r3(acc[:, 0:30], 2), func=AF.Copy),
                    reads=[acck], writes=['USh%d' % m])
            for c in range(4):
                acc, acck = acc_tile()
                S.op('pe', lambda e, acc=acc, c=c: e.transpose(
                    out=acc[:, 0:60], in_=STB[0:60, c * 128:(c + 1) * 128], identity=ID32[0:60, 0:60]),
                    reads=['STB', 'ID32'], writes=[acck])
                S.op('act', lambda e, acc=acc, c=c: e.activation(
                    out=VS32[:, c, :, 2:32], in_=r3(acc[:, 0:60], 2), func=AF.Copy, scale=2.0),
                    reads=[acck], writes=['VS32h'] if c == 3 else ['VS32h_%d' % c])

        def stagePOOL(i):
            s2 = i % 2
            PL = PLS[s2]
            if is_h(i):
                def uu(m, lo, hi):
                    return US[:, m, :, lo:hi]

                def pa(lo, hi):
                    return r3(PA[:, 0:64], 2)[:, :, lo:hi]

                def pb(lo, hi):
                    return r3(PB[:, 0:64], 2)[:, :, lo:hi]
                W = 32
                ukeys = lambda m: ['USn%d' % m, 'USh%d' % m]
                plv = lambda m: r3(PL[:, m, 0:32], 2)
            else:
                def uu(m, lo, hi):
                    return U[s2][:, m, lo:hi]

                def pa(lo, hi):
                    return PA[:, lo:hi]

                def pb(lo, hi):
                    return PB[:, lo:hi]
                W = 16 + NT
                ukeys = lambda m: ['U%dc%d' % (s2, m), 'U%dm' % s2]
                plv = lambda m: PL[:, m, :]

            def add(out, a, b, reads, writes):
                S.op('dve', lambda e: e.tensor_tensor(out=out, in0=a, in1=b, op=ALU.add),
                     reads=reads, writes=writes)
            for m, w in enumerate(WINS):
                uk = ukeys(m)
                if w == 2:
                    add(pa(16, W), uu(m, 16, W), uu(m, 15, W - 1), uk, ['PA'])
                    fin, fk = pa, 'PA'
                elif w == 4:
                    add(pa(14, W), uu(m, 14, W), uu(m, 13, W - 1), uk, ['PA'])
                    add(pb(16, W), pa(16, W), pa(14, W - 2), ['PA'], ['PB'])
                    fin, fk = pb, 'PB'
                elif w == 8:
                    add(pa(10, W), uu(m, 10, W), uu(m, 9, W - 1), uk, ['PA'])
                    add(pb(12, W), pa(12, W), pa(10, W - 2), ['PA'], ['PB'])
                    add(pa(16, W), pb(16, W), pb(12, W - 4), ['PB'], ['PA'])
                    fin, fk = pa, 'PA'
                else:
                    add(pa(2, W), uu(m, 2, W), uu(m, 1, W - 1), uk, ['PA'])
                    add(pb(4, W), pa(4, W), pa(2, W - 2), ['PA'], ['PB'])
                    add(pa(8, W), pb(8, W), pb(4, W - 4), ['PB'], ['PA'])
                    add(pb(16, W), pa(16, W), pa(8, W - 8), ['PA'], ['PB'])
                    fin, fk = pb, 'PB'
                S.op('dve', lambda e, m=m, w=w, fin=fin: e.scalar_tensor_tensor(
                    out=plv(m), in0=fin(16, W), scalar=1.0 / w, in1=uu(m, 16, W),
                    op0=ALU.mult, op1=ALU.subtract), reads=[fk] + uk, writes=['PL%d_%d' % (s2, m)])
                if i == 1:
                    S.op('dve', lambda e, m=m, fin=fin: e.tensor_tensor(
                        out=T16[:], in0=fin(16, 32), in1=INV[:, m, :], op=ALU.mult),
                        reads=[fk, 'INV%d' % m], writes=['T16'])
                    S.op('dve', lambda e, m=m: e.tensor_tensor(
                        out=PL[:, m, 0:16], in0=T16[:], in1=uu(m, 16, 32), op=ALU.subtract),
                        reads=['T16'] + uk, writes=['PL%d_%d' % (s2, m)])

        def stagePM(i):
            s2 = i % 2
            n = 32 if is_h(i) else NT
            for m in range(4):
                acc, acck = acc_tile()
                S.op('pe', lambda e, acc=acc, m=m: e.matmul(
                    acc[:, 0:n], lhsT=PMW[:, m, :], rhs=PLS[s2][:, m, 0:n], start=True, stop=True),
                    reads=['PMW', 'PL%d_%d' % (s2, m)], writes=[acck])
                S.op('dve', lambda e, acc=acc, m=m: e.scalar_tensor_tensor(
                    out=MIXP[s2][:, m, 0:n], in0=acc[:, 0:n], scalar=PP[:, m:m + 1], in1=SGP[s2][:, m, 0:n],
                    op0=ALU.mult, op1=ALU.mult), reads=[acck, 'PP', 'SGP%d_%d' % (s2, m)],
                    writes=['MIXP%d_%d' % (s2, m)])

        def stageCONV(i, pm=True):
            s2 = i % 2
            n = 32 if is_h(i) else NT
            pend = None

            def stats(c, cb):
                S.op('pe', lambda e: e.matmul(STP[:, 0:2 * n], lhsT=ONES[:], rhs=CBS[cb][:, :, 0:n],
                                              start=(c == 0), stop=(c == 3)),
                     reads=['ONES', 'CBSa%d' % cb, 'CBSb%d' % cb], writes=['STP'])
            for c in range(4):
                cv, cvk = cv_tile()
                for r in range(8):
                    for q in range(4):
                        j = 4 * c + q
                        if is_h(i):
                            S.op('pe', lambda e, cv=cv, j=j, r=r, q=q, c=c: e.matmul(
                                r3(cv[32 * q:32 * q + 32, 0:32], 2), lhsT=WS[:, j, r, :],
                                rhs=VRH[:, q, c, :, 1 + r:1 + r + 16],
                                start=(r == 0), stop=(r == 7), tile_position=(0, 32 * q)),
                                reads=['WS', 'VRH'], writes=[cvk])
                        else:
                            S.op('pe', lambda e, cv=cv, j=j, r=r, q=q: e.matmul(
                                cv[32 * q:32 * q + 32, :], lhsT=WS[:, j, r, :], rhs=VRS[s2][:, j, 1 + r:1 + r + NT],
                                start=(r == 0), stop=(r == 7), tile_position=(0, 32 * q)),
                                reads=['WS', 'VRa%d' % s2], writes=[cvk])
                if pend is not None:
                    stats(*pend)
                cb = cnt['cb'] % 2
                cnt['cb'] += 1
                S.op('act', lambda e, cv=cv, c=c: e.activation(
                    out=CO[:, c, 0:n], in_=cv[:, 0:n], func=AF.Identity, bias=PP[:, 4 + c:5 + c], scale=1.0),
                    reads=[cvk, 'PP'], writes=['CO%d' % c])
                S.op('act', lambda e, cv=cv, c=c, cb=cb: e.activation(
                    out=CBS[cb][:, 1, 0:n], in_=cv[:, 0:n], func=AF.Square, bias=PP[:, 4 + c:5 + c], scale=1.0),
                    reads=[cvk, 'PP'], writes=['CBSb%d' % cb])
                S.op('act', lambda e, cv=cv, c=c, cb=cb: e.activation(
                    out=CBS[cb][:, 0, 0:n], in_=cv[:, 0:n], func=AF.Identity, bias=PP[:, 4 + c:5 + c], scale=1.0),
                    reads=[cvk, 'PP'], writes=['CBSa%d' % cb])
                pend = (c, cb)
            if pm:
                stagePM(i)
            stats(*pend)

        ln_state = {}

        def stageLN_a(i):
            n = 32 if is_h(i) else NT
            cok = ['CO%d' % c for c in range(4)]
            S.op('act', lambda e: e.activation(out=MUS[:, 0:n], in_=STP[:, 0:n], func=AF.Copy),
                 reads=['STP'], writes=['MUS'])
            S.op('dve', lambda e: e.tensor_tensor(out=MSQ[:, 0:n], in0=MUS[:, 0:n], in1=MUS[:, 0:n], op=ALU.mult),
                 reads=['MUS'], writes=['MSQ'])
            S.op('dve', lambda e: e.tensor_tensor(out=MSQ[:, 0:n], in0=STP[:, n:2 * n], in1=MSQ[:, 0:n],
                                                  op=ALU.subtract), reads=['STP', 'MSQ'], writes=['MSQ'])
            tw = 32 if is_h(i) else 128
            nsub = 1 if is_h(i) else NT // 128
            for j in range(nsub):
                c0 = (i * 2 + j) * 2
                vt = LNS[:, c0:c0 + 1]
                rt = LNS[:, c0 + 1:c0 + 2]
                vk = 'LNS%d' % c0
                rk = 'LNS%d' % (c0 + 1)
                S.op('dve', lambda e, j=j, vt=vt: e.scalar_tensor_tensor(
                    out=DSC[:, 0:tw], in0=MSQ[:, j * tw:(j + 1) * tw], scalar=1.0, in1=ID32[:, 0:tw],
                    op0=ALU.mult, op1=ALU.mult, accum_out=vt), reads=['MSQ', 'ID32', 'LNS'], writes=['DSC', vk])
                S.op('dve', lambda e, vt=vt: e.tensor_scalar(out=vt, in0=vt, scalar1=EPS_LN, scalar2=None,
                                                           op0=ALU.add), reads=[vk], writes=[vk])
                S.op('pool', lambda e, vt=vt, rt=rt: e.tensor_tensor(out=rt, in0=vt, in1=NEGH[:, 0:1], op=ALU.pow),
                     reads=[vk, 'NEGH', 'LNS'], writes=[rk])

        def stageLN_b(i):
            n = 32 if is_h(i) else NT
            cok = ['CO%d' % c for c in range(4)]
            tw = 32 if is_h(i) else 128
            nsub = 1 if is_h(i) else NT // 128
            rsb, rsbk = STP[:, 0:NT], 'STP'
            for j in range(nsub):
                c0 = (i * 2 + j) * 2
                rt = LNS[:, c0 + 1:c0 + 2]
                rk = 'LNS%d' % (c0 + 1)
                S.op('pe', lambda e, j=j, rt=rt, rsb=rsb: e.transpose(
                    out=rsb[:, j * tw:(j + 1) * tw], in_=rt[0:tw, :].to_broadcast([tw, 128]),
                    identity=ID32[0:tw, 0:tw]), reads=[rk, 'ID32'], writes=[rsbk])

        def stageLN_c(i):
            n = 32 if is_h(i) else NT
            cok = ['CO%d' % c for c in range(4)]
            rsb, rsbk = STP[:, 0:NT], 'STP'
            S.op('dve', lambda e: e.tensor_tensor(
                out=CO[:, :, 0:n], in0=CO[:, :, 0:n], in1=MUS[:, 0:n].unsqueeze(1).to_broadcast([128, 4, n]),
                op=ALU.subtract), reads=cok + ['MUS'], writes=cok)
            S.op('dve', lambda e: e.tensor_tensor(
                out=CO[:, :, 0:n], in0=CO[:, :, 0:n], in1=rsb[:, 0:n].unsqueeze(1).to_broadcast([128, 4, n]),
                op=ALU.mult), reads=cok + [rsbk], writes=cok)
            for c in range(4):
                S.op('act', lambda e, c=c: e.activation(
                    out=LNO[:, c, 0:n], in_=CO[:, c, 0:n], func=AF.Silu,
                    scale=PP[:, 8 + c:9 + c], bias=PP[:, 12 + c:13 + c]),
                    reads=['CO%d' % c, 'PP'], writes=['LNO%d' % c])

        def stagePW(i):
            s2 = i % 2
            n = 32 if is_h(i) else NT
            for m2 in range(4):
                acc, acck = acc_tile()
                for c in range(4):
                    S.op('pe', lambda e, acc=acc, c=c, m2=m2: e.matmul(
                        acc[:, 0:n], lhsT=PWW[:, c, m2 * 128:(m2 + 1) * 128], rhs=LNO[:, c, 0:n],
                        start=(c == 0), stop=(c == 3)),
                        reads=['PWW', 'LNO%d' % c], writes=[acck])
                S.op('dve', lambda e, acc=acc, m2=m2: e.scalar_tensor_tensor(
                    out=MIXC[:, m2, 0:n], in0=acc[:, 0:n], scalar=PP[:, 16 + m2:17 + m2],
                    in1=SGC[s2][:, m2, 0:n], op0=ALU.add, op1=ALU.mult),
                    reads=[acck, 'PP', 'SGC%d_%d' % (s2, m2)], writes=['MIXC%d' % m2])

        out_subs = {}

        def stageOUT(i, js=None, fin=True):
            s2 = i % 2
            s3 = i % 3
            g = i - 1
            tw = 32 if is_h(i) else 128
            subs = out_subs.setdefault(i, [])
            for j in (range(1 if is_h(i) else 2) if js is None else js):
                xk = 'X%dj%d' % (s3, j)
                for nh in range(2):
                    o, ok = o_tile()
                    for e_ in range(8):
                        lh = MIXP[s2][:, e_, j * tw:(j + 1) * tw] if e_ < 4 else MIXC[:, e_ - 4, j * tw:(j + 1) * tw]
                        lk = ('MIXP%d_%d' % (s2, e_)) if e_ < 4 else ('MIXC%d' % (e_ - 4))
                        S.op('pe', lambda e, o=o, lh=lh, e_=e_, nh=nh: e.matmul(
                            o[0:tw, :], lhsT=lh, rhs=WOUT[:, e_, nh * 512:(nh + 1) * 512],
                            start=(e_ == 0), stop=(e_ == 7)), reads=[lk, 'WOUTh%d' % (e_ // 4)], writes=[ok])
                    S.op('dve', lambda e, o=o, j=j, nh=nh: e.tensor_tensor(
                        out=XB[s3][0:tw, j, nh * 512:(nh + 1) * 512], in0=o[0:tw, :],
                        in1=XB[s3][0:tw, j, nh * 512:(nh + 1) * 512], op=ALU.add),
                        reads=[ok, xk], writes=[xk])
                subs.append((j, xk, statc(i, j, 2), statc(i, j, 3), XB[s3][0:tw, j, :]))
            if not fin:
                return
            for j, xk, (ss, ssk), (rs, rsk), yt in subs:
                jt, jk = junk_tile()
                S.op('act', lambda e, yt=yt, ss=ss, jt=jt: e.activation(
                    out=jt[0:tw, :], in_=yt, func=AF.Square, scale=1.0 / 32.0, accum_out=ss[0:tw, :]),
                    reads=[xk, 'STAT'], writes=[ssk, jk])
            for j, xk, (ss, ssk), (rs, rsk), yt in subs:
                S.op('dve', lambda e, ss=ss: e.tensor_scalar(
                    out=ss[0:tw, :], in0=ss[0:tw, :], scalar1=EPS_RMS, scalar2=None, op0=ALU.add),
                    reads=[ssk], writes=[ssk])
            for j, xk, (ss, ssk), (rs, rsk), yt in subs:
                S.op('pool', lambda e, ss=ss, rs=rs: e.tensor_tensor(
                    out=rs[0:tw, :], in0=ss[0:tw, :], in1=NEGH[0:tw, 0:1], op=ALU.pow),
                    reads=[ssk, 'NEGH', 'STAT'], writes=[rsk])
            for j, xk, (ss, ssk), (rs, rsk), yt in subs:
                S.op('dve', lambda e, yt=yt, rs=rs: e.scalar_tensor_tensor(
                    out=yt, in0=yt, scalar=rs[0:tw, :], in1=FGB[0:tw, :], op0=ALU.mult, op1=ALU.mult),
                    reads=[xk, rsk, 'FGB'], writes=[xk])
                if is_h(i):
                    S.op('sp', lambda e, yt=yt: e.dma_start(out=ys_d, in_=yt), reads=[xk], writes=['o_ys'],
                         dma='st_s')
                else:
                    dst = yp_d[g * NT + j * 128:g * NT + (j + 1) * 128, :]
                    S.op('sp', lambda e, yt=yt, dst=dst: e.dma_start(out=dst, in_=yt), reads=[xk],
                         writes=['o_yp%d_%d' % (i, j)], dma='st%d' % ((i * 2 + j) % 4))

        def state_rows(src, nrows, scale, dst, stg, stgk, reads, okey, chan):
            o, ok = o_tile()
            for c in range(4):
                S.op('pe', lambda e, c=c: e.transpose(
                    out=o[0:nrows, c * 128:(c + 1) * 128], in_=src(c), identity=ID32[:]),
                    reads=reads(c) + ['ID32'], writes=[ok])
            S.op('act', lambda e: e.activation(out=stg[0:nrows, :], in_=o[0:nrows, :], func=AF.Copy, scale=scale),
                 reads=[ok], writes=[stgk])
            S.op('sp', lambda e: e.dma_start(out=dst, in_=stg[0:nrows, :]), reads=[stgk], writes=[okey], dma=chan)

        def state_out(kind):
            if kind == 'ss_pool':
                for s in range(2):
                    state_rows(lambda c, s=s: US[:, c, s, 17:32], 15, 1.0, ss_pool_d[s * 15:(s + 1) * 15, :],
                               (STA, STB)[s], ('STA', 'STB')[s], lambda c: ['USn%d' % c], 'o_ssp%d' % s,
                               ('st_a', 'st_b')[s])
            elif kind == 'ss_conv':
                for s in range(2):
                    state_rows(lambda c, s=s: VS32[:, c, s, 18:48], 30, 0.5, ss_conv_d[s * 30:(s + 1) * 30, :],
                               (STA, STB)[s], ('STA', 'STB')[s], lambda c: ['VS32n%d' % c, 'VS32h'],
                               'o_ssc%d' % s, ('st_a', 'st_b')[s])
            elif kind == 'sp_pool':
                s2 = (NPOS - 1) % 2
                state_rows(lambda c: U[s2][:, c, 16 + NT - 15:16 + NT], 15, 1.0, sp_pool_d, STA, 'STA',
                           lambda c: ['U%dc%d' % (s2, c)], 'o_spp', 'st_a')
            else:
                state_rows(lambda c: V32T[:, c, 2:32], 30, 0.5, sp_conv_d, STB, 'STB',
                           lambda c: ['V32T%d' % c], 'o_spc', 'st_b')

        load_x(1)
        stageI_pre(0)
        stageI_pre(1)
        load_x(2)
        stageI_pe(0)
        stageI_pe(1)
        build_inv()
        stageI_pre(2)
        load_misc_weights_a()
        build_ws()
        for sg in range(3):
            if sg == 1:
                cache_prep()
            stageP(0, segs=(sg,), tail=(sg == 1))
            if sg == 1:
                replicate_v(0)
            if sg == 2:
                stagePOOL(0)
            stageP(1, rep=False, segs=(sg,), tail=(sg == 2))
            if sg == 1:
                replicate_v(1)
            if sg == 2:
                stagePOOL(1)
        stageP(0, segs=(3,), tail=False)
        stageP(1, rep=False, segs=(3,), tail=False)
        stageP(0, segs=(4,), tail=False)
        stageP(1, rep=False, segs=(4,), tail=False)
        stageI_pe(2)
        stageCONV(0)
        stageLN_a(0)
        load_wout()
        stageP(2, rep=False, segs=(0,), tail=False)
        stageP(2, rep=True, segs=(1,), tail=False)
        stageLN_b(0)
        stageLN_c(0)
        for i in range(NPOS):
            if i + 2 < NPOS and i + 2 > 2:
                load_x(i + 2)
            if i == 1:
                stageP(2, rep=False, segs=(2,), tail=False)
            elif i + 1 < NPOS and i >= 1:
                stageP(i + 1, segs=(0, 1, 2), tail=False)
            if i >= 1 and i != NPOS - 1:
                stageLN_c(i)
            if i + 1 < NPOS and i >= 1:
                stageP(i + 1, segs=(3, 4), tail=True)
                stagePOOL(i + 1)
            if i != NPOS - 1:
                stagePW(i)
            if i == NPOS - 2:
                state_out('sp_pool')
                state_out('sp_conv')
            if i + 2 < NPOS and i + 2 > 2:
                stageI_pre(i + 2)
            if i + 1 < NPOS:
                stageCONV(i + 1)
                stageLN_a(i + 1)
            if i + 2 < NPOS and i + 2 > 2:
                stageI_pe(i + 2)
            if i == NPOS - 2:
                stageOUT(i, js=(0,), fin=False)
                stageLN_b(i + 1)
                stageLN_c(i + 1)
                stageOUT(i, js=(1,), fin=False)
                stagePW(i + 1)
                stageOUT(i, js=(), fin=True)
            else:
                stageOUT(i)
                if i + 1 < NPOS:
                    stageLN_b(i + 1)
            if i == 0:
                state_out('ss_pool')
                state_out('ss_conv')
        outs = ['o_ys', 'o_ssp0', 'o_ssp1', 'o_ssc0', 'o_ssc1', 'o_spp', 'o_spc']
        outs += ['o_yp%d_%d' % (i, j) for i in range(1, NPOS) for j in range(2)]
        S.op('sp', None, reads=outs)
        info = S.emit(st)
    return nc, info


_CACHE = {}


def _get_program():
    if 'nc' not in _CACHE:
        _CACHE['nc'], _CACHE['info'] = build_program()
    return _CACHE['nc']


def _pack_pp(inp, pos0):
    pp = np.zeros((128, NPP), np.float32)

    def col4(v):
        return np.ascontiguousarray(np.asarray(v, np.float32).reshape(4, 128).T)
    pp[:, 0:4] = col4(inp['pool_scale'][0])
    pp[:, 4:8] = col4(inp['dw_b'][0])
    pp[:, 8:12] = col4(inp['ln_g'][0])
    pp[:, 12:16] = col4(inp['ln_b'][0])
    pp[:, 16:20] = col4(inp['pw_b'][0])
    dw = np.asarray(inp['dw_w'][0], np.float32)
    pp[:, 20:144] = dw.T.reshape(4, 128, NTAP).transpose(1, 0, 2).reshape(128, 4 * NTAP)
    pp[:, 144:152] = np.asarray(inp['norm_g'][0], np.float32).reshape(8, 128).T
    pp[:, 152] = pos0
    dwp = np.concatenate([np.zeros((1, 512), np.float32), dw], axis=0)
    pp[:, 153:281] = dwp.reshape(4, 8, 16, 32).transpose(0, 3, 2, 1).reshape(128, 128)
    pp[:, 281:313] = np.tile(np.eye(32, dtype=np.float32), (4, 1))
    return pp


def kernel(**inputs):
    inp = {k: np.asarray(v) for k, v in inputs.items()}
    xpr = np.asarray(inp['x_prompt'], np.float32)
    xs = np.asarray(inp['x_sample'], np.float32)
    cpool = np.asarray(inp['cache_pool'], np.float32)[0]
    cconv = np.asarray(inp['cache_conv'], np.float32)[0]
    nc = _get_program()
    shared = dict(
        w_in=np.ascontiguousarray(inp['w_in'][0], np.float32),
        w_out=np.ascontiguousarray(inp['w_out'][0], np.float32),
        pw_w=np.ascontiguousarray(inp['pw_w'][0], np.float32),
        pool_mix=np.ascontiguousarray(np.asarray(inp['pool_mix'][0], np.float32).reshape(512, 128)),
        final_g=np.ascontiguousarray(np.asarray(inp['final_g'], np.float32).reshape(1, 1024)),
        norm_g=np.ascontiguousarray(np.asarray(inp['norm_g'], np.float32).reshape(1, 1024)),
        ident=np.eye(128, dtype=np.float32),
    )
    in_maps = []
    for c in range(8):
        b, s = c // 2, c % 2
        xp = np.ascontiguousarray(xpr[b, s * 2048:(s + 1) * 2048])
        xh = np.zeros((64, 1024), np.float32)
        xh[0:32] = xs[2 * c:2 * c + 2].reshape(32, 1024)
        if s == 1:
            xh[32:64] = xpr[b, 2048 - 32:2048]
        m = dict(shared)
        m.update(xp=xp, xh=xh,
                 cpool=np.ascontiguousarray(cpool[2 * c:2 * c + 2].reshape(30, 512)),
                 cconv=np.ascontiguousarray(cconv[2 * c:2 * c + 2].reshape(60, 512)),
                 pp=_pack_pp(inp, float(s * 2048)))
        in_maps.append(m)
    res = run_bass_kernel_spmd(nc, in_maps, core_ids=list(range(8)))
    r = res.results
    y_prompt = np.empty((4, 4096, 1024), np.float32)
    y_sample = np.empty((16, 16, 1024), np.float32)
    sp_pool = np.empty((1, 4, 15, 512), np.float32)
    sp_conv = np.empty((1, 4, 30, 512), np.float32)
    ss_pool = np.empty((1, 16, 15, 512), np.float32)
    ss_conv = np.empty((1, 16, 30, 512), np.float32)
    for c in range(8):
        b, s = c // 2, c % 2
        y_prompt[b, s * 2048:(s + 1) * 2048] = r[c]['yp']
        y_sample[2 * c:2 * c + 2] = np.asarray(r[c]['ys']).reshape(2, 16, 1024)
        ss_pool[0, 2 * c:2 * c + 2] = np.asarray(r[c]['ss_pool']).reshape(2, 15, 512)
        ss_conv[0, 2 * c:2 * c + 2] = np.asarray(r[c]['ss_conv']).reshape(2, 30, 512)
        if s == 1:
            sp_pool[0, b] = r[c]['sp_pool']
            sp_conv[0, b] = r[c]['sp_conv']
    return (y_prompt, y_sample, sp_pool, sp_conv, ss_pool, ss_conv)
```

```python
from contextlib import ExitStack

import numpy as np
import concourse.bass as bass
import concourse.mybir as mybir
from concourse.bass_utils import run_bass_kernel_spmd

F32 = mybir.dt.float32
BF16 = mybir.dt.bfloat16
ALU = mybir.AluOpType
AF = mybir.ActivationFunctionType

NT = 256
NPOS = 9
WINS = (2, 4, 8, 16)
NTAP = 31
EPS_RMS = 1e-6
EPS_LN = 1e-5
NPP = 313


class Sched:
    def __init__(self, nc):
        self.nc = nc
        self.ops = []
        self.last_w = {}
        self.readers = {}
        self.last_dma = {}
        self.cur_batch = {}
        self.batch_last = {}

    def op(self, eng, fn, reads=(), writes=(), dma=None, batch=None):
        idx = len(self.ops)
        deps = set()
        for k in reads:
            if k in self.last_w:
                deps.add(self.last_w[k])
        for k in writes:
            if k in self.last_w:
                deps.add(self.last_w[k])
            deps.update(self.readers.get(k, ()))
        same_batch = (batch is not None and self.cur_batch.get(dma) == batch)
        if dma is not None and dma in self.last_dma and not same_batch:
            deps.add(self.last_dma[dma])
        deps.discard(idx)
        self.ops.append(dict(eng=eng, fn=fn, deps=deps, dma=dma, batch=batch))
        if batch is not None:
            self.cur_batch[dma] = batch
            self.batch_last[(dma, batch)] = idx
        if dma is not None:
            self.last_dma[dma] = idx
        for k in writes:
            self.last_w[k] = idx
            self.readers[k] = set()
        for k in reads:
            self.readers.setdefault(k, set()).add(idx)
        return idx

    def emit(self, stack):
        nc = self.nc
        engs = {'pe': nc.tensor, 'act': nc.scalar, 'dve': nc.vector,
                'pool': nc.gpsimd, 'sp': nc.sync}
        ops = self.ops
        pos = {}
        cnt = {}
        for i, o in enumerate(ops):
            st = ('dma', o['dma']) if o['dma'] is not None else o['eng']
            o['stream'] = st
            cnt[st] = cnt.get(st, 0) + 1
            pos[i] = cnt[st]
        waited = {}
        signal = set()
        for i, o in enumerate(ops):
            need = {}
            for d in o['deps']:
                if ops[d]['batch'] is not None:
                    if o['batch'] == ops[d]['batch'] and o['dma'] == ops[d]['dma']:
                        continue
                    d = self.batch_last[(ops[d]['dma'], ops[d]['batch'])]
                sd = ops[d]['stream']
                if sd == 'pe' and o['eng'] == 'pe' and o['dma'] is None:
                    continue
                if sd not in need or pos[d] > pos[need[sd]]:
                    need[sd] = d
            w = []
            for sd, d in need.items():
                key = (o['eng'], sd)
                if waited.get(key, 0) < pos[d]:
                    waited[key] = pos[d]
                    w.append(d)
                    signal.add(d)
            o['waits'] = w
        val = {}
        run = {}
        for i, o in enumerate(ops):
            st = o['stream']
            if isinstance(st, tuple):
                run[st] = run.get(st, 0) + 16
                val[i] = run[st]
            elif i in signal:
                run[st] = run.get(st, 0) + 1
                val[i] = run[st]
        sems = {}
        for st in cnt:
            nm = ('d_' + st[1]) if isinstance(st, tuple) else ('e_' + st)
            sems[st] = stack.enter_context(nc.semaphore(nm))
        nwait = 0
        for i, o in enumerate(ops):
            e = engs[o['eng']]
            ws = list(o['waits'])
            attach = ws.pop() if (ws and o['fn'] is not None) else None
            for d in ws:
                e.wait_ge(sems[ops[d]['stream']], val[d])
                nwait += 1
            if o['fn'] is None:
                continue
            ins = o['fn'](e)
            if attach is not None:
                ins._wait_ge(sems[ops[attach]['stream']], val[attach])
            st = o['stream']
            if isinstance(st, tuple):
                ins.then_inc(sems[st], 16)
            elif i in signal:
                ins.then_inc(sems[st], 1)
        return dict(n_ops=len(ops), n_waits=nwait, n_signal=len(signal))


def build_program():
    nc = bass.Bass("TRN2", target_bir_lowering=False)

    def din(name, shape):
        return nc.dram_tensor(name, list(shape), F32, kind="ExternalInput").ap()

    def dout(name, shape):
        return nc.dram_tensor(name, list(shape), F32, kind="ExternalOutput").ap()

    xp_d = din("xp", (2048, 1024))
    xh_d = din("xh", (64, 1024))
    cpool_d = din("cpool", (30, 512))
    cconv_d = din("cconv", (60, 512))
    win_d = din("w_in", (1024, 2560))
    wout_d = din("w_out", (1024, 1024))
    pww_d = din("pw_w", (512, 512))
    pmw_d = din("pool_mix", (512, 128))
    fg_d = din("final_g", (1, 1024))
    ng_d = din("norm_g", (1, 1024))
    pp_d = din("pp", (128, NPP))
    id_d = din("ident", (128, 128))
    yp_d = dout("yp", (2048, 1024))
    ys_d = dout("ys", (32, 1024))
    sp_pool_d = dout("sp_pool", (15, 512))
    sp_conv_d = dout("sp_conv", (30, 512))
    ss_pool_d = dout("ss_pool", (30, 512))
    ss_conv_d = dout("ss_conv", (60, 512))

    with ExitStack() as st:
        def sb(n, s, d):
            return st.enter_context(nc.sbuf_tensor(n, list(s), d))

        def ps(n, s, d):
            return st.enter_context(nc.psum_tensor(n, list(s), d))

        WIN = sb("WIN", (128, 8, 2560), BF16)
        WOUT = sb("WOUT", (128, 8, 1024), BF16)
        PWW = sb("PWW", (128, 4, 512), BF16)
        PMW = sb("PMW", (128, 4, 128), BF16)
        WS = sb("WS", (128, 16, 8, 32), BF16)
        VRS = [sb("VR%d" % s_, (128, 16, NT + 8), BF16) for s_ in range(2)]
        VRH = sb("VRH", (128, 4, 4, 2, 48), BF16)
        GB = sb("GB", (128, 1024), F32)
        ONES = sb("ONES", (128, 128), BF16)
        ID32 = sb("ID32", (128, 128), F32)
        IDB = sb("IDB", (128, 128), BF16)
        FGB = sb("FGB", (128, 1024), F32)
        PP = sb("PP", (128, NPP), F32)
        DWS = sb("DWS", (128, 128), F32)
        NEGH = sb("NEGH", (128, 8), F32)
        STAT = sb("STAT", (128, 80), F32)
        IOT = sb("IOT", (128, 16), F32)
        CNT = sb("CNT", (128, 16), F32)
        INV = sb("INV", (128, 4, 16), F32)
        T16 = sb("T16", (128, 16), F32)
        XB = [sb("XB%d" % s, (128, 2, 1024), F32) for s in range(3)]
        JUNKS = [sb("JUNK%d" % j, (128, 1024), BF16) for j in range(1)]
        HG = [sb("HG%d" % h, (128, 1024), BF16) for h in range(3)]
        HT = sb("HT", (128, 8, NT), BF16)
        HTH = sb("HTH", (128, 8, 64), BF16)
        THH = sb("THH", (128, 4, 64), F32)
        U = [sb("U%d" % s, (128, 4, 16 + NT), F32) for s in range(2)]
        V = [sb("V%d" % s, (128, 4, 32 + NT), BF16) for s in range(2)]
        US = sb("US", (128, 4, 2, 32), F32)
        VS32 = sb("VS32", (128, 4, 2, 48), F32)
        VSB = sb("VSB", (128, 4, 2, 48), BF16)
        V32T = sb("V32T", (128, 4, 32), F32)
        SGP = [sb("SGP%d" % s, (128, 4, NT), F32) for s in range(2)]
        SGC = [sb("SGC%d" % s, (128, 4, NT), F32) for s in range(2)]
        TH = [sb("TH%d" % s, (128, NT), F32) for s in range(4)]
        PA = sb("PA", (128, 16 + NT), F32)
        PB = sb("PB", (128, 16 + NT), F32)
        PLS = [sb("PL%d" % s, (128, 4, NT), BF16) for s in range(2)]
        CO = sb("CO", (128, 4, NT), F32)
        CBS = [sb("CBS%d" % s, (128, 2, NT), BF16) for s in range(2)]
        MUS = sb("MUS", (128, NT), F32)
        MSQ = sb("MSQ", (128, NT), F32)
        LNS = sb("LNS", (128, 64), F32)
        DSC = sb("DSC", (128, 128), F32)
        LNO = sb("LNO", (128, 4, NT), BF16)
        MIXP = [sb("MIXP%d" % s, (128, 4, NT), BF16) for s in range(2)]
        MIXC = sb("MIXC", (128, 4, NT), BF16)
        STA = sb("STA", (64, 512), F32)
        STB = sb("STB", (64, 512), F32)
        GEN = [ps("GEN%d" % s, (128, 512), F32) for s in range(5)]
        GENB = [g.bitcast(BF16) for g in GEN]
        STP = ps("STP", (128, 512), F32)
        OB = [ps("OB%d" % s, (128, 512), F32) for s in range(2)]

        S = Sched(nc)
        cnt = dict(acc=0, cv=0, o=0, tp=0, hg=0, th=0, stg=0, cb=0, junk=0)

        def junk_tile():
            return JUNKS[0], 'JUNK0'

        def acc_tile():
            i = cnt['acc'] % 5
            cnt['acc'] += 1
            return GEN[i][:, 0:NT], "GEN%d" % i

        def tp_tile():
            i = cnt['acc'] % 5
            cnt['acc'] += 1
            return GENB[i], "GEN%d" % i

        cv_tile = acc_tile

        def o_tile():
            i = cnt['o'] % 2
            cnt['o'] += 1
            return OB[i], "O%d" % i

        def statc(i, j, kind):
            c = (i * 2 + j) * 4 + kind
            return STAT[:, c:c + 1], "ST%d" % c

        def r3(ap, a):
            return ap.rearrange("p (a b) -> p a b", a=a)

        def wload(dst, src, key, chan):
            S.op('pool', lambda e: e.dma_start(out=dst, in_=src), writes=[key], dma=chan)

        def load_win():
            wv = win_d.rearrange("(k p) e -> p k e", p=128)
            n_ = 0
            for seg in (3, 2, 0, 1, 4):
                for hf in range(2):
                    lo = seg * 512 + hf * 256
                    wload(WIN[:, :, lo:lo + 256], wv[:, :, lo:lo + 256], 'WINs%d_%d' % (seg, hf), 'w%d' % n_)
                    n_ += 1

        def load(dst, src, key, chan):
            S.op('sp', lambda e: e.dma_start(out=dst, in_=src), writes=[key], dma=chan)

        load_win()
        load(PP[:], pp_d, 'PP', 'c_pp')
        load(ID32[:], id_d, 'ID32', 'c_id')
        load(XB[0][0:64, 0, :], xh_d, 'X0j0', 'ldx0')
        S.op('pool', lambda e: e.memset(STAT[:], 0.0), writes=['STAT'])
        S.op('pool', lambda e: e.memset(NEGH[:], -0.5), writes=['NEGH'])
        S.op('pool', lambda e: e.memset(LNS[:], 0.0), writes=['LNS'])
        S.op('pool', lambda e: e.memset(VS32[:], 0.0), writes=['VS32h'] + ['VS32n%d' % c for c in range(4)] + ['VS32h_%d' % c for c in range(3)])
        S.op('pool', lambda e: e.memset(US[:], 0.0), writes=['USn%d' % c for c in range(4)] + ['USh%d' % c for c in range(4)])
        S.op('pool', lambda e: e.memset(ONES[:], 1.0 / 512.0), writes=['ONES'])
        S.op('pool', lambda e: e.iota(IOT[:], [[1, 16]], base=0, channel_multiplier=0,
                                      allow_small_or_imprecise_dtypes=True), writes=['IOT'])
        S.op('dve', lambda e: e.tensor_copy(out=IDB[:], in_=ID32[:]), reads=['ID32'], writes=['IDB'])
        S.op('dve', lambda e: e.tensor_scalar(out=DWS[:], in0=PP[:, 153:281], scalar1=0.5, scalar2=None,
                                              op0=ALU.mult), reads=['PP'], writes=['DWS'])
        SETUP_LATE = [lambda: load(GB[:], ng_d.partition_broadcast(128), 'GB', 'c_gb'),
                      lambda: load(STA[0:30, :], cpool_d, 'STA', 'c_sta'),
                      lambda: load(STB[0:60, :], cconv_d, 'STB', 'c_stb')]
        S.op('dve', lambda e: e.tensor_scalar(out=CNT[:], in0=IOT[:], scalar1=PP[:, 152:153], scalar2=1.0,
                                              op0=ALU.add, op1=ALU.add), reads=['IOT', 'PP'], writes=['CNT'])
        for m, w in enumerate(WINS):
            S.op('dve', lambda e, m=m, w=w: e.tensor_scalar(out=INV[:, m, :], in0=CNT[:], scalar1=float(w),
                                                            scalar2=None, op0=ALU.min),
                 reads=['CNT'], writes=['INV%d' % m])
            S.op('dve', lambda e, m=m: e.reciprocal(out=INV[:, m, :], in_=INV[:, m, :]),
                 reads=['INV%d' % m], writes=['INV%d' % m])

        PROJ_ORDER = [12, 13, 14, 15, 8, 9, 10, 11, 0, 1, 2, 3, 4, 5, 6, 7, 16, 17, 18, 19]

        def load_misc_weights_a():
            wload(PMW[:], pmw_d.rearrange("(g p) d -> p g d", p=128), 'PMW', 'w0')
            wload(PWW[:], pww_d.rearrange("(c p) e -> p c e", p=128), 'PWW', 'w1')

        def load_wout():
            wo = wout_d.rearrange("(k p) e -> p k e", p=128)
            for h in range(2):
                wload(WOUT[:, 4 * h:4 * h + 4, :], wo[:, 4 * h:4 * h + 4, :], 'WOUTh%d' % h, 'w%d' % (2 - h))
            load(FGB[:], fg_d.partition_broadcast(128), 'FGB', 'c_fg')

        def build_ws():
            S.op('dve', lambda e: e.tensor_tensor(
                out=WS[:].rearrange("p j r m -> p (j r) m"),
                in0=PP[:, 281:313].unsqueeze(1).to_broadcast([128, 128, 32]),
                in1=DWS[:].unsqueeze(2).to_broadcast([128, 128, 32]), op=ALU.mult),
                reads=['PP', 'DWS'], writes=['WS'])

        def replicate_v(i):
            s2 = i % 2
            for sh in range(4):
                for q in range(4):
                    if is_h(i):
                        src = VSB[32 * q:32 * q + 32, :, :, 8 * sh:8 * sh + 24].rearrange("p c s x -> p (c s) x")
                        dst = VRH[32 * sh:32 * sh + 32, q, :, :, 0:24].rearrange("p c s x -> p (c s) x")
                        rk, wk, ch = ['VSB'], 'VRH', 'vrh'
                    else:
                        src = V[s2][32 * q:32 * q + 32, :, 8 * sh:8 * sh + NT + 8]
                        dst = VRS[s2][32 * sh:32 * sh + 32, :, :].rearrange("p (c q) x -> p c q x", q=4)[:, :, q, :]
                        rk = ['V%dc%d' % (s2, c) for c in range(4)] + ['V%dm' % s2]
                        wk, ch = 'VRa%d' % s2, 'vrp'
                    S.op('pool', lambda e, src=src, dst=dst: e.dma_start(out=dst, in_=src),
                         reads=rk, writes=[wk], dma=ch, batch=i)

        def is_h(i):
            return i == 0

        def load_x(i):
            g = i - 1
            s = i % 3
            src = xp_d[g * NT:(g + 1) * NT, :].rearrange("(j p) d -> p j d", p=128)
            if i in (1, 2):
                for j, eng in ((0, 'sp'), (1, 'act')):
                    S.op(eng, lambda e, j=j: e.dma_start(out=XB[s][:, j, :], in_=src[:, j, :]),
                         writes=['X%dj%d' % (s, j)], dma='ldx%d_%d' % (s, j))
                return
            S.op('sp', lambda e: e.dma_start(out=XB[s][:], in_=src),
                 writes=['X%dj0' % s, 'X%dj1' % s], dma='ldx%d' % s)

        hg_of = {}

        def stageI_pre(i):
            s = i % 3
            npart = 64 if is_h(i) else 128
            subs = []
            for j in range(1 if is_h(i) else 2):
                h = cnt['hg'] % 3
                cnt['hg'] += 1
                hg_of[(i, j)] = h
                subs.append((j, XB[s][0:npart, j, :], 'X%dj%d' % (s, j), statc(i, j, 0), statc(i, j, 1), h))
            for j, xt, xk, (ss, ssk), (rs, rsk), h in subs:
                jt, jk = junk_tile()
                S.op('act', lambda e, xt=xt, ss=ss, jt=jt: e.activation(
                    out=jt[0:npart, :], in_=xt, func=AF.Square, scale=1.0 / 32.0, accum_out=ss[0:npart, :]),
                    reads=[xk, 'STAT'], writes=[ssk, jk])
            for j, xt, xk, (ss, ssk), (rs, rsk), h in subs:
                S.op('dve', lambda e, ss=ss: e.tensor_scalar(
                    out=ss[0:npart, :], in0=ss[0:npart, :], scalar1=EPS_RMS, scalar2=None, op0=ALU.add),
                    reads=[ssk], writes=[ssk])
            for j, xt, xk, (ss, ssk), (rs, rsk), h in subs:
                S.op('pool', lambda e, ss=ss, rs=rs: e.tensor_tensor(
                    out=rs[0:npart, :], in0=ss[0:npart, :], in1=NEGH[0:npart, 0:1], op=ALU.pow),
                    reads=[ssk, 'NEGH', 'STAT'], writes=[rsk])
            for j, xt, xk, (ss, ssk), (rs, rsk), h in subs:
                S.op('dve', lambda e, xt=xt, rs=rs, h=h: e.scalar_tensor_tensor(
                    out=HG[h][0:npart, :], in0=xt, scalar=rs[0:npart, :], in1=GB[0:npart, :],
                    op0=ALU.mult, op1=ALU.mult), reads=[xk, rsk, 'GB'], writes=['HG%d' % h])

        def stageI_pe(i):
            nsub = 1 if is_h(i) else 2
            tw = 64 if is_h(i) else 128
            for j in range(nsub):
                h = hg_of[(i, j)]
                for half in range(1 if is_h(i) else 2):
                    nk = 8 if is_h(i) else 4
                    i_ = cnt['acc'] % 5
                    cnt['acc'] += 1
                    tp = GEN[i_]
                    tpk = "GEN%d" % i_
                    for kk in range(nk):
                        k = half * nk + kk
                        S.op('pe', lambda e, k=k, kk=kk, h=h, tp=tp: e.matmul(
                            tp[:, kk * tw:(kk + 1) * tw], lhsT=HG[h][0:tw, k * 128:(k + 1) * 128],
                            rhs=IDB[0:tw, 0:tw], start=True, stop=True),
                            reads=['HG%d' % h, 'IDB'], writes=[tpk])
                    hts = HTH if is_h(i) else HT
                    S.op('act', lambda e, tp=tp, j=j, half=half, nk=nk, hts=hts: e.activation(
                        out=hts[:, half * nk:(half + 1) * nk, j * tw:(j + 1) * tw], in_=r3(tp[:, 0:nk * tw], nk),
                        func=AF.Copy), reads=[tpk], writes=['HTH'] if is_h(i) else ['HTj%d_%d' % (j, half)])

        th_of = {}

        def stageP(i, rep=True, segs=(0, 1, 2, 3, 4), tail=True):
            n = 64 if is_h(i) else NT
            s2 = i % 2
            sn = (i + 1) % 2
            htk = ['HTH'] if is_h(i) else ['HTj0_0', 'HTj0_1', 'HTj1_0', 'HTj1_1']
            hts = HTH if is_h(i) else HT
            last = (i == NPOS - 1)
            th_cur = th_of.setdefault(i, {})
            if tuple(segs) == (0, 1, 2):
                order = [12, 8, 13, 9, 14, 10, 15, 11, 0, 1, 2, 3]
            elif len(segs) == 5:
                order = [12, 8, 13, 9, 14, 10, 15, 11] + PROJ_ORDER[8:]
            else:
                order = [m_ for sg in segs for m_ in PROJ_ORDER[4 * sg:4 * sg + 4]]
            for m in order:
                acc, acck = acc_tile()
                for k in range(8):
                    S.op('pe', lambda e, k=k, m=m, acc=acc: e.matmul(
                        acc[:, 0:n], lhsT=WIN[:, k, m * 128:(m + 1) * 128], rhs=hts[:, k, 0:n],
                        start=(k == 0), stop=(k == 7)), reads=htk + ['WINs%d_%d' % (m // 4, (m % 4) // 2)], writes=[acck])
                if 12 <= m < 16:
                    c = m - 12
                    if is_h(i):
                        t = 'H%d' % c
                        tht = THH[:, c, :]
                    else:
                        t = cnt['th'] % 4
                        cnt['th'] += 1
                        tht = TH[t]
                    th_cur[c] = (t, tht)
                    S.op('act', lambda e, acc=acc, tht=tht: e.activation(
                        out=tht[:, 0:n], in_=acc[:, 0:n], func=AF.Tanh, scale=0.5),
                        reads=[acck], writes=['TH%s' % t])
                elif 8 <= m < 12:
                    c = m - 8
                    t, tht = th_cur[c]
                    if is_h(i):
                        S.op('dve', lambda e, acc=acc, tht=tht, c=c: e.scalar_tensor_tensor(
                            out=VS32[:, c, :, 32:48], in0=r3(tht[:, 0:32], 2), scalar=1.0,
                            in1=r3(acc[:, 0:32], 2), op0=ALU.add, op1=ALU.mult),
                            reads=[acck, 'TH%s' % t], writes=['VS32n%d' % c])
                        S.op('dve', lambda e, acc=acc, tht=tht, c=c: e.scalar_tensor_tensor(
                            out=V[sn][:, c, 0:32], in0=tht[:, 32:64], scalar=1.0,
                            in1=acc[:, 32:64], op0=ALU.add, op1=ALU.mult),
                            reads=[acck, 'TH%s' % t], writes=['V%dm' % sn])
                    else:
                        S.op('dve', lambda e, acc=acc, tht=tht, c=c: e.scalar_tensor_tensor(
                            out=V[s2][:, c, 32:32 + NT], in0=tht[:], scalar=1.0,
                            in1=acc, op0=ALU.add, op1=ALU.mult),
                            reads=[acck, 'TH%s' % t], writes=['V%dc%d' % (s2, c)])
                        if last:
                            S.op('dve', lambda e, acc=acc, tht=tht, c=c: e.scalar_tensor_tensor(
                                out=V32T[:, c, :], in0=tht[:, NT - 32:NT], scalar=1.0,
                                in1=acc[:, NT - 32:NT], op0=ALU.add, op1=ALU.mult),
                                reads=[acck, 'TH%s' % t], writes=['V32T%d' % c])
                    if m == 11 and rep and not is_h(i):
                        replicate_v(i)
                elif m < 4:
                    if is_h(i):
                        S.op('act', lambda e, acc=acc, m=m: e.activation(
                            out=US[:, m, :, 16:32], in_=r3(acc[:, 0:32], 2), func=AF.Copy),
                            reads=[acck], writes=['USn%d' % m])
                        S.op('act', lambda e, acc=acc, m=m: e.activation(
                            out=U[sn][:, m, 0:16], in_=acc[:, 48:64], func=AF.Copy),
                            reads=[acck], writes=['U%dm' % sn])
                    else:
                        S.op('act', lambda e, acc=acc, m=m: e.activation(
                            out=U[s2][:, m, 16:16 + NT], in_=acc, func=AF.Copy),
                            reads=[acck], writes=['U%dc%d' % (s2, m)])
                elif m < 8:
                    mm_ = m - 4
                    nn = 32 if is_h(i) else NT
                    S.op('act', lambda e, acc=acc, mm_=mm_, nn=nn: e.activation(
                        out=SGP[s2][:, mm_, 0:nn], in_=acc[:, 0:nn], func=AF.Silu),
                        reads=[acck], writes=['SGP%d_%d' % (s2, mm_)])
                else:
                    mm_ = m - 16
                    nn = 32 if is_h(i) else NT
                    S.op('act', lambda e, acc=acc, mm_=mm_, nn=nn: e.activation(
                        out=SGC[s2][:, mm_, 0:nn], in_=acc[:, 0:nn], func=AF.Silu),
                        reads=[acck], writes=['SGC%d_%d' % (s2, mm_)])
            if not tail:
                return
            if is_h(i):
                S.op('dve', lambda e: e.tensor_copy(out=VSB[:], in_=VS32[:]),
                     reads=['VS32n%d' % c for c in range(4)] + ['VS32h'], writes=['VSB'])
            if is_h(i):
                pass
            elif not last:
                S.op('pool', lambda e: e.tensor_copy(out=V[sn][:, :, 0:32], in_=V[s2][:, :, NT:NT + 32]),
                     reads=['V%dc%d' % (s2, c) for c in range(4)], writes=['V%dm' % sn])
                S.op('pool', lambda e: e.tensor_copy(out=U[sn][:, :, 0:16], in_=U[s2][:, :, NT:NT + 16]),
                     reads=['U%dc%d' % (s2, c) for c in range(4)], writes=['U%dm' % sn])

        def cache_prep():
            for m in range(4):
                acc, acck = acc_tile()
                S.op('pe', lambda e, acc=acc, m=m: e.transpose(
                    out=acc[:, 0:30], in_=STA[0:30, m * 128:(m + 1) * 128], identity=ID32[0:30, 0:30]),
                    reads=['STA', 'ID32'], writes=[acck])
                S.op('act', lambda e, acc=acc, m=m: e.activation(
                    out=US[:, m, :, 1:16], in_=r3(acc[:, 0:30], 2), func=AF.Copy),
                    reads=[acck], writes=['USh%d' % m])
            for c in range(4):
                acc, acck = acc_tile()
                S.op('pe', lambda e, acc=acc, c=c: e.transpose(
                    out=acc[:, 0:60], in_=STB[0:60, c * 128:(c + 1) * 128], identity=ID32[0:60, 0:60]),
                    reads=['STB', 'ID32'], writes=[acck])
                S.op('act', lambda e, acc=acc, c=c: e.activation(
                    out=VS32[:, c, :, 2:32], in_=r3(acc[:, 0:60], 2), func=AF.Copy, scale=2.0),
                    reads=[acck], writes=['VS32h'] if c == 3 else ['VS32h_%d' % c])

        def stagePOOL(i):
            s2 = i % 2
            PL = PLS[s2]
            if is_h(i):
                def uu(m, lo, hi):
                    return US[:, m, :, lo:hi]

                def pa(lo, hi):
                    return r3(PA[:, 0:64], 2)[:, :, lo:hi]

                def pb(lo, hi):
                    return r3(PB[:, 0:64], 2)[:, :, lo:hi]
                W = 32
                ukeys = lambda m: ['USn%d' % m, 'USh%d' % m]
                plv = lambda m: r3(PL[:, m, 0:32], 2)
            else:
                def uu(m, lo, hi):
                    return U[s2][:, m, lo:hi]

                def pa(lo, hi):
                    return PA[:, lo:hi]

                def pb(lo, hi):
                    return PB[:, lo:hi]
                W = 16 + NT
                ukeys = lambda m: ['U%dc%d' % (s2, m), 'U%dm' % s2]
                plv = lambda m: PL[:, m, :]

            def add(out, a, b, reads, writes):
                S.op('dve', lambda e: e.tensor_tensor(out=out, in0=a, in1=b, op=ALU.add),
                     reads=reads, writes=writes)
            for m, w in enumerate(WINS):
                uk = ukeys(m)
                if w == 2:
                    add(pa(16, W), uu(m, 16, W), uu(m, 15, W - 1), uk, ['PA'])
                    fin, fk = pa, 'PA'
                elif w == 4:
                    add(pa(14, W), uu(m, 14, W), uu(m, 13, W - 1), uk, ['PA'])
                    add(pb(16, W), pa(16, W), pa(14, W - 2), ['PA'], ['PB'])
                    fin, fk = pb, 'PB'
                elif w == 8:
                    add(pa(10, W), uu(m, 10, W), uu(m, 9, W - 1), uk, ['PA'])
                    add(pb(12, W), pa(12, W), pa(10, W - 2), ['PA'], ['PB'])
                    add(pa(16, W), pb(16, W), pb(12, W - 4), ['PB'], ['PA'])
                    fin, fk = pa, 'PA'
                else:
                    add(pa(2, W), uu(m, 2, W), uu(m, 1, W - 1), uk, ['PA'])
                    add(pb(4, W), pa(4, W), pa(2, W - 2), ['PA'], ['PB'])
                    add(pa(8, W), pb(8, W), pb(4, W - 4), ['PB'], ['PA'])
                    add(pb(16, W), pa(16, W), pa(8, W - 8), ['PA'], ['PB'])
                    fin, fk = pb, 'PB'
                S.op('dve', lambda e, m=m, w=w, fin=fin: e.scalar_tensor_tensor(
                    out=plv(m), in0=fin(16, W), scalar=1.0 / w, in1=uu(m, 16, W),
                    op0=ALU.mult, op1=ALU.subtract), reads=[fk] + uk, writes=['PL%d_%d' % (s2, m)])
                if i == 1:
                    S.op('dve', lambda e, m=m, fin=fin: e.tensor_tensor(
                        out=T16[:], in0=fin(16, 32), in1=INV[:, m, :], op=ALU.mult),
                        reads=[fk, 'INV%d' % m], writes=['T16'])
                    S.op('dve', lambda e, m=m: e.tensor_tensor(
                        out=PL[:, m, 0:16], in0=T16[:], in1=uu(m, 16, 32), op=ALU.subtract),
                        reads=['T16'] + uk, writes=['PL%d_%d' % (s2, m)])

        def stagePM(i):
            s2 = i % 2
            n = 32 if is_h(i) else NT
            for m in range(4):
                acc, acck = acc_tile()
                S.op('pe', lambda e, acc=acc, m=m: e.matmul(
                    acc[:, 0:n], lhsT=PMW[:, m, :], rhs=PLS[s2][:, m, 0:n], start=True, stop=True),
                    reads=['PMW', 'PL%d_%d' % (s2, m)], writes=[acck])
                S.op('dve', lambda e, acc=acc, m=m: e.scalar_tensor_tensor(
                    out=MIXP[s2][:, m, 0:n], in0=acc[:, 0:n], scalar=PP[:, m:m + 1], in1=SGP[s2][:, m, 0:n],
                    op0=ALU.mult, op1=ALU.mult), reads=[acck, 'PP', 'SGP%d_%d' % (s2, m)],
                    writes=['MIXP%d_%d' % (s2, m)])

        def stageCONV(i, pm=True):
            s2 = i % 2
            n = 32 if is_h(i) else NT
            pend = None

            def stats(c, cb):
                S.op('pe', lambda e: e.matmul(STP[:, 0:2 * n], lhsT=ONES[:], rhs=CBS[cb][:, :, 0:n],
                                              start=(c == 0), stop=(c == 3)),
                     reads=['ONES', 'CBSa%d' % cb, 'CBSb%d' % cb], writes=['STP'])
            for c in range(4):
                cv, cvk = cv_tile()
                for r in range(8):
                    for q in range(4):
                        j = 4 * c + q
                        if is_h(i):
                            S.op('pe', lambda e, cv=cv, j=j, r=r, q=q, c=c: e.matmul(
                                r3(cv[32 * q:32 * q + 32, 0:32], 2), lhsT=WS[:, j, r, :],
                                rhs=VRH[:, q, c, :, 1 + r:1 + r + 16],
                                start=(r == 0), stop=(r == 7), tile_position=(0, 32 * q)),
                                reads=['WS', 'VRH'], writes=[cvk])
                        else:
                            S.op('pe', lambda e, cv=cv, j=j, r=r, q=q: e.matmul(
                                cv[32 * q:32 * q + 32, :], lhsT=WS[:, j, r, :], rhs=VRS[s2][:, j, 1 + r:1 + r + NT],
                                start=(r == 0), stop=(r == 7), tile_position=(0, 32 * q)),
                                reads=['WS', 'VRa%d' % s2], writes=[cvk])
                if pend is not None:
                    stats(*pend)
                cb = cnt['cb'] % 2
                cnt['cb'] += 1
                S.op('act', lambda e, cv=cv, c=c: e.activation(
                    out=CO[:, c, 0:n], in_=cv[:, 0:n], func=AF.Identity, bias=PP[:, 4 + c:5 + c], scale=1.0),
                    reads=[cvk, 'PP'], writes=['CO%d' % c])
                S.op('act', lambda e, cv=cv, c=c, cb=cb: e.activation(
                    out=CBS[cb][:, 1, 0:n], in_=cv[:, 0:n], func=AF.Square, bias=PP[:, 4 + c:5 + c], scale=1.0),
                    reads=[cvk, 'PP'], writes=['CBSb%d' % cb])
                S.op('act', lambda e, cv=cv, c=c, cb=cb: e.activation(
                    out=CBS[cb][:, 0, 0:n], in_=cv[:, 0:n], func=AF.Identity, bias=PP[:, 4 + c:5 + c], scale=1.0),
                    reads=[cvk, 'PP'], writes=['CBSa%d' % cb])
                pend = (c, cb)
            if pm:
                stagePM(i)
            stats(*pend)

        ln_state = {}

        def stageLN_a(i):
            n = 32 if is_h(i) else NT
            cok = ['CO%d' % c for c in range(4)]
            S.op('act', lambda e: e.activation(out=MUS[:, 0:n], in_=STP[:, 0:n], func=AF.Copy),
                 reads=['STP'], writes=['MUS'])
            S.op('dve', lambda e: e.tensor_tensor(out=MSQ[:, 0:n], in0=MUS[:, 0:n], in1=MUS[:, 0:n], op=ALU.mult),
                 reads=['MUS'], writes=['MSQ'])
            S.op('dve', lambda e: e.tensor_tensor(out=MSQ[:, 0:n], in0=STP[:, n:2 * n], in1=MSQ[:, 0:n],
                                                  op=ALU.subtract), reads=['STP', 'MSQ'], writes=['MSQ'])
            tw = 32 if is_h(i) else 128
            nsub = 1 if is_h(i) else NT // 128
            for j in range(nsub):
                c0 = (i * 2 + j) * 2
                vt = LNS[:, c0:c0 + 1]
                rt = LNS[:, c0 + 1:c0 + 2]
                vk = 'LNS%d' % c0
                rk = 'LNS%d' % (c0 + 1)
                S.op('dve', lambda e, j=j, vt=vt: e.scalar_tensor_tensor(
                    out=DSC[:, 0:tw], in0=MSQ[:, j * tw:(j + 1) * tw], scalar=1.0, in1=ID32[:, 0:tw],
                    op0=ALU.mult, op1=ALU.mult, accum_out=vt), reads=['MSQ', 'ID32', 'LNS'], writes=['DSC', vk])
                S.op('dve', lambda e, vt=vt: e.tensor_scalar(out=vt, in0=vt, scalar1=EPS_LN, scalar2=None,
                                                           op0=ALU.add), reads=[vk], writes=[vk])
                S.op('pool', lambda e, vt=vt, rt=rt: e.tensor_tensor(out=rt, in0=vt, in1=NEGH[:, 0:1], op=ALU.pow),
                     reads=[vk, 'NEGH', 'LNS'], writes=[rk])

        def stageLN_b(i):
            n = 32 if is_h(i) else NT
            cok = ['CO%d' % c for c in range(4)]
            tw = 32 if is_h(i) else 128
            nsub = 1 if is_h(i) else NT // 128
            rsb, rsbk = STP[:, 0:NT], 'STP'
            for j in range(nsub):
                c0 = (i * 2 + j) * 2
                rt = LNS[:, c0 + 1:c0 + 2]
                rk = 'LNS%d' % (c0 + 1)
                S.op('pe', lambda e, j=j, rt=rt, rsb=rsb: e.transpose(
                    out=rsb[:, j * tw:(j + 1) * tw], in_=rt[0:tw, :].to_broadcast([tw, 128]),
                    identity=ID32[0:tw, 0:tw]), reads=[rk, 'ID32'], writes=[rsbk])

        def stageLN_c(i):
            n = 32 if is_h(i) else NT
            cok = ['CO%d' % c for c in range(4)]
            rsb, rsbk = STP[:, 0:NT], 'STP'
            S.op('dve', lambda e: e.tensor_tensor(
                out=CO[:, :, 0:n], in0=CO[:, :, 0:n], in1=MUS[:, 0:n].unsqueeze(1).to_broadcast([128, 4, n]),
                op=ALU.subtract), reads=cok + ['MUS'], writes=cok)
            S.op('dve', lambda e: e.tensor_tensor(
                out=CO[:, :, 0:n], in0=CO[:, :, 0:n], in1=rsb[:, 0:n].unsqueeze(1).to_broadcast([128, 4, n]),
                op=ALU.mult), reads=cok + [rsbk], writes=cok)
            for c in range(4):
                S.op('act', lambda e, c=c: e.activation(
                    out=LNO[:, c, 0:n], in_=CO[:, c, 0:n], func=AF.Silu,
                    scale=PP[:, 8 + c:9 + c], bias=PP[:, 12 + c:13 + c]),
                    reads=['CO%d' % c, 'PP'], writes=['LNO%d' % c])

        def stagePW(i):
            s2 = i % 2
            n = 32 if is_h(i) else NT
            for m2 in range(4):
                acc, acck = acc_tile()
                for c in range(4):
                    S.op('pe', lambda e, acc=acc, c=c, m2=m2: e.matmul(
                        acc[:, 0:n], lhsT=PWW[:, c, m2 * 128:(m2 + 1) * 128], rhs=LNO[:, c, 0:n],
                        start=(c == 0), stop=(c == 3)),
                        reads=['PWW', 'LNO%d' % c], writes=[acck])
                S.op('dve', lambda e, acc=acc, m2=m2: e.scalar_tensor_tensor(
                    out=MIXC[:, m2, 0:n], in0=acc[:, 0:n], scalar=PP[:, 16 + m2:17 + m2],
                    in1=SGC[s2][:, m2, 0:n], op0=ALU.add, op1=ALU.mult),
                    reads=[acck, 'PP', 'SGC%d_%d' % (s2, m2)], writes=['MIXC%d' % m2])

        out_subs = {}

        def stageOUT(i, js=None, fin=True):
            s2 = i % 2
            s3 = i % 3
            g = i - 1
            tw = 32 if is_h(i) else 128
            subs = out_subs.setdefault(i, [])
            for j in (range(1 if is_h(i) else 2) if js is None else js):
                xk = 'X%dj%d' % (s3, j)
                for nh in range(2):
                    o, ok = o_tile()
                    for e_ in range(8):
                        lh = MIXP[s2][:, e_, j * tw:(j + 1) * tw] if e_ < 4 else MIXC[:, e_ - 4, j * tw:(j + 1) * tw]
                        lk = ('MIXP%d_%d' % (s2, e_)) if e_ < 4 else ('MIXC%d' % (e_ - 4))
                        S.op('pe', lambda e, o=o, lh=lh, e_=e_, nh=nh: e.matmul(
                            o[0:tw, :], lhsT=lh, rhs=WOUT[:, e_, nh * 512:(nh + 1) * 512],
                            start=(e_ == 0), stop=(e_ == 7)), reads=[lk, 'WOUTh%d' % (e_ // 4)], writes=[ok])
                    S.op('dve', lambda e, o=o, j=j, nh=nh: e.tensor_tensor(
                        out=XB[s3][0:tw, j, nh * 512:(nh + 1) * 512], in0=o[0:tw, :],
                        in1=XB[s3][0:tw, j, nh * 512:(nh + 1) * 512], op=ALU.add),
                        reads=[ok, xk], writes=[xk])
                subs.append((j, xk, statc(i, j, 2), statc(i, j, 3), XB[s3][0:tw, j, :]))
            if not fin:
                return
            for j, xk, (ss, ssk), (rs, rsk), yt in subs:
                jt, jk = junk_tile()
                S.op('act', lambda e, yt=yt, ss=ss, jt=jt: e.activation(
                    out=jt[0:tw, :], in_=yt, func=AF.Square, scale=1.0 / 32.0, accum_out=ss[0:tw, :]),
                    reads=[xk, 'STAT'], writes=[ssk, jk])
            for j, xk, (ss, ssk), (rs, rsk), yt in subs:
                S.op('dve', lambda e, ss=ss: e.tensor_scalar(
                    out=ss[0:tw, :], in0=ss[0:tw, :], scalar1=EPS_RMS, scalar2=None, op0=ALU.add),
                    reads=[ssk], writes=[ssk])
            for j, xk, (ss, ssk), (rs, rsk), yt in subs:
                S.op('pool', lambda e, ss=ss, rs=rs: e.tensor_tensor(
                    out=rs[0:tw, :], in0=ss[0:tw, :], in1=NEGH[0:tw, 0:1], op=ALU.pow),
                    reads=[ssk, 'NEGH', 'STAT'], writes=[rsk])
            for j, xk, (ss, ssk), (rs, rsk), yt in subs:
                S.op('dve', lambda e, yt=yt, rs=rs: e.scalar_tensor_tensor(
                    out=yt, in0=yt, scalar=rs[0:tw, :], in1=FGB[0:tw, :], op0=ALU.mult, op1=ALU.mult),
                    reads=[xk, rsk, 'FGB'], writes=[xk])
                if is_h(i):
                    S.op('sp', lambda e, yt=yt: e.dma_start(out=ys_d, in_=yt), reads=[xk], writes=['o_ys'],
                         dma='st_s')
                else:
                    dst = yp_d[g * NT + j * 128:g * NT + (j + 1) * 128, :]
                    S.op('sp', lambda e, yt=yt, dst=dst: e.dma_start(out=dst, in_=yt), reads=[xk],
                         writes=['o_yp%d_%d' % (i, j)], dma='st%d' % ((i * 2 + j) % 4))

        def state_rows(src, nrows, scale, dst, stg, stgk, reads, okey, chan):
            o, ok = o_tile()
            for c in range(4):
                S.op('pe', lambda e, c=c: e.transpose(
                    out=o[0:nrows, c * 128:(c + 1) * 128], in_=src(c), identity=ID32[:]),
                    reads=reads(c) + ['ID32'], writes=[ok])
            S.op('act', lambda e: e.activation(out=stg[0:nrows, :], in_=o[0:nrows, :], func=AF.Copy, scale=scale),
                 reads=[ok], writes=[stgk])
            S.op('sp', lambda e: e.dma_start(out=dst, in_=stg[0:nrows, :]), reads=[stgk], writes=[okey], dma=chan)

        def state_out(kind):
            if kind == 'ss_pool':
                for s in range(2):
                    state_rows(lambda c, s=s: US[:, c, s, 17:32], 15, 1.0, ss_pool_d[s * 15:(s + 1) * 15, :],
                               (STA, STB)[s], ('STA', 'STB')[s], lambda c: ['USn%d' % c], 'o_ssp%d' % s,
                               ('st_a', 'st_b')[s])
            elif kind == 'ss_conv':
                for s in range(2):
                    state_rows(lambda c, s=s: VS32[:, c, s, 18:48], 30, 0.5, ss_conv_d[s * 30:(s + 1) * 30, :],
                               (STA, STB)[s], ('STA', 'STB')[s], lambda c: ['VS32n%d' % c, 'VS32h'],
                               'o_ssc%d' % s, ('st_a', 'st_b')[s])
            elif kind == 'sp_pool':
                s2 = (NPOS - 1) % 2
                state_rows(lambda c: U[s2][:, c, 16 + NT - 15:16 + NT], 15, 1.0, sp_pool_d, STA, 'STA',
                           lambda c: ['U%dc%d' % (s2, c)], 'o_spp', 'st_a')
            else:
                state_rows(lambda c: V32T[:, c, 2:32], 30, 0.5, sp_conv_d, STB, 'STB',
                           lambda c: ['V32T%d' % c], 'o_spc', 'st_b')

        load_x(1)
        for f in SETUP_LATE:
            f()
        stageI_pre(0)
        stageI_pre(1)
        load_x(2)
        stageI_pe(0)
        stageI_pe(1)
        stageI_pre(2)
        load_misc_weights_a()
        build_ws()
        for sg in range(3):
            if sg == 1:
                cache_prep()
            stageP(0, segs=(sg,), tail=(sg == 1))
            if sg == 1:
                replicate_v(0)
            if sg == 2:
                stagePOOL(0)
            stageP(1, rep=False, segs=(sg,), tail=(sg == 2))
            if sg == 1:
                replicate_v(1)
            if sg == 2:
                stagePOOL(1)
        stageP(0, segs=(3,), tail=False)
        stageP(1, rep=False, segs=(3,), tail=False)
        stageP(0, segs=(4,), tail=False)
        stageP(1, rep=False, segs=(4,), tail=False)
        stageI_pe(2)
        stageCONV(0)
        stageLN_a(0)
        load_wout()
        stageP(2, rep=False, segs=(0,), tail=False)
        stageP(2, rep=True, segs=(1,), tail=False)
        stageLN_b(0)
        stageLN_c(0)
        for i in range(NPOS):
            if i + 2 < NPOS and i + 2 > 2:
                load_x(i + 2)
            if i == 1:
                stageP(2, rep=False, segs=(2,), tail=False)
            elif i + 1 < NPOS and i >= 1:
                stageP(i + 1, segs=(0, 1, 2), tail=False)
            if i >= 1 and i != NPOS - 1:
                stageLN_c(i)
            if i + 1 < NPOS and i >= 1:
                stageP(i + 1, segs=(3, 4), tail=True)
                stagePOOL(i + 1)
            if i != NPOS - 1:
                stagePW(i)
            if i == NPOS - 2:
                state_out('sp_pool')
                state_out('sp_conv')
            if i + 2 < NPOS and i + 2 > 2:
                stageI_pre(i + 2)
            if i + 1 < NPOS:
                stageCONV(i + 1)
                stageLN_a(i + 1)
            if i + 2 < NPOS and i + 2 > 2:
                stageI_pe(i + 2)
            if i == NPOS - 2:
                stageOUT(i, js=(0,), fin=False)
                stageLN_b(i + 1)
                stageLN_c(i + 1)
                stageOUT(i, js=(1,), fin=False)
                stagePW(i + 1)
                stageOUT(i, js=(), fin=True)
            else:
                stageOUT(i)
                if i + 1 < NPOS:
                    stageLN_b(i + 1)
            if i == 0:
                state_out('ss_pool')
                state_out('ss_conv')
        outs = ['o_ys', 'o_ssp0', 'o_ssp1', 'o_ssc0', 'o_ssc1', 'o_spp', 'o_spc']
        outs += ['o_yp%d_%d' % (i, j) for i in range(1, NPOS) for j in range(2)]
        S.op('sp', None, reads=outs)
        info = S.emit(st)
    return nc, info


_CACHE = {}


def _get_program():
    if 'nc' not in _CACHE:
        _CACHE['nc'], _CACHE['info'] = build_program()
    return _CACHE['nc']


def _pack_pp(inp, pos0):
    pp = np.zeros((128, NPP), np.float32)

    def col4(v):
        return np.ascontiguousarray(np.asarray(v, np.float32).reshape(4, 128).T)
    pp[:, 0:4] = col4(inp['pool_scale'][0])
    pp[:, 4:8] = col4(inp['dw_b'][0])
    pp[:, 8:12] = col4(inp['ln_g'][0])
    pp[:, 12:16] = col4(inp['ln_b'][0])
    pp[:, 16:20] = col4(inp['pw_b'][0])
    dw = np.asarray(inp['dw_w'][0], np.float32)
    pp[:, 20:144] = dw.T.reshape(4, 128, NTAP).transpose(1, 0, 2).reshape(128, 4 * NTAP)
    pp[:, 144:152] = np.asarray(inp['norm_g'][0], np.float32).reshape(8, 128).T
    pp[:, 152] = pos0
    dwp = np.concatenate([np.zeros((1, 512), np.float32), dw], axis=0)
    pp[:, 153:281] = dwp.reshape(4, 8, 16, 32).transpose(0, 3, 2, 1).reshape(128, 128)
    pp[:, 281:313] = np.tile(np.eye(32, dtype=np.float32), (4, 1))
    return pp


def kernel(**inputs):
    inp = {k: np.asarray(v) for k, v in inputs.items()}
    xpr = np.asarray(inp['x_prompt'], np.float32)
    xs = np.asarray(inp['x_sample'], np.float32)
    cpool = np.asarray(inp['cache_pool'], np.float32)[0]
    cconv = np.asarray(inp['cache_conv'], np.float32)[0]
    nc = _get_program()
    shared = dict(
        w_in=np.ascontiguousarray(inp['w_in'][0], np.float32),
        w_out=np.ascontiguousarray(inp['w_out'][0], np.float32),
        pw_w=np.ascontiguousarray(inp['pw_w'][0], np.float32),
        pool_mix=np.ascontiguousarray(np.asarray(inp['pool_mix'][0], np.float32).reshape(512, 128)),
        final_g=np.ascontiguousarray(np.asarray(inp['final_g'], np.float32).reshape(1, 1024)),
        norm_g=np.ascontiguousarray(np.asarray(inp['norm_g'], np.float32).reshape(1, 1024)),
        ident=np.eye(128, dtype=np.float32),
    )
    in_maps = []
    for c in range(8):
        b, s = c // 2, c % 2
        xp = np.ascontiguousarray(xpr[b, s * 2048:(s + 1) * 2048])
        xh = np.zeros((64, 1024), np.float32)
        xh[0:32] = xs[2 * c:2 * c + 2].reshape(32, 1024)
        if s == 1:
            xh[32:64] = xpr[b, 2048 - 32:2048]
        m = dict(shared)
        m.update(xp=xp, xh=xh,
                 cpool=np.ascontiguousarray(cpool[2 * c:2 * c + 2].reshape(30, 512)),
                 cconv=np.ascontiguousarray(cconv[2 * c:2 * c + 2].reshape(60, 512)),
                 pp=_pack_pp(inp, float(s * 2048)))
        in_maps.append(m)
    res = run_bass_kernel_spmd(nc, in_maps, core_ids=list(range(8)))
    r = res.results
    y_prompt = np.empty((4, 4096, 1024), np.float32)
    y_sample = np.empty((16, 16, 1024), np.float32)
    sp_pool = np.empty((1, 4, 15, 512), np.float32)
    sp_conv = np.empty((1, 4, 30, 512), np.float32)
    ss_pool = np.empty((1, 16, 15, 512), np.float32)
    ss_conv = np.empty((1, 16, 30, 512), np.float32)
    for c in range(8):
        b, s = c // 2, c % 2
        y_prompt[b, s * 2048:(s + 1) * 2048] = r[c]['yp']
        y_sample[2 * c:2 * c + 2] = np.asarray(r[c]['ys']).reshape(2, 16, 1024)
        ss_pool[0, 2 * c:2 * c + 2] = np.asarray(r[c]['ss_pool']).reshape(2, 15, 512)
        ss_conv[0, 2 * c:2 * c + 2] = np.asarray(r[c]['ss_conv']).reshape(2, 30, 512)
        if s == 1:
            sp_pool[0, b] = r[c]['sp_pool']
            sp_conv[0, b] = r[c]['sp_conv']
    return (y_prompt, y_sample, sp_pool, sp_conv, ss_pool, ss_conv)
```
